# Optimizing a Trainium2 kernel written in Bass

```python
import math
import jax, jax.numpy as jnp
from jax import lax
import numpy as np

D_MODEL = 1024
BATCH = 8
SEQ = 4096
DEPTH = 1

EPS = 1e-6
D_FF = 2816
CONV_CH = D_MODEL
CONV_WIDTH = 31
HEAD_DIM = 64
N_Q_HEADS = 16
N_KV_HEADS = 4
GROUP = N_Q_HEADS // N_KV_HEADS
WINDOW = 128
BLOCK = WINDOW
N_BUCKETS = 32
MAX_DISTANCE = 128
Q_W = N_Q_HEADS * HEAD_DIM
KV_W = N_KV_HEADS * HEAD_DIM
SPLITS = (2 * CONV_CH, Q_W, KV_W, KV_W, D_MODEL, D_MODEL)
IN_W = sum(SPLITS)

kernel_name = "hybrid_conformer_conv_swa_sink_macaron"


def rmsnorm(x, g):
    xf = x.astype(jnp.float32)
    y = xf * lax.rsqrt(jnp.mean(xf * xf, axis=-1, keepdims=True) + EPS)
    return (y * g.astype(jnp.float32)).astype(x.dtype)


def layernorm(x, g, b):
    xf = x.astype(jnp.float32)
    mu = jnp.mean(xf, axis=-1, keepdims=True)
    var = jnp.mean(jnp.square(xf - mu), axis=-1, keepdims=True)
    y = (xf - mu) * lax.rsqrt(var + EPS)
    return (y * g.astype(jnp.float32) + b.astype(jnp.float32)).astype(x.dtype)


def swiglu_ffn(x, w_in, w_out):
    a, b = jnp.split(x @ w_in, 2, axis=-1)
    return (jax.nn.silu(a) * b) @ w_out


def t5_causal_bucket(dist):
    max_exact = N_BUCKETS // 2
    d = jnp.maximum(dist, 1).astype(jnp.float32)
    large = max_exact + (jnp.log(d / max_exact) / math.log(MAX_DISTANCE / max_exact)
                         * (N_BUCKETS - max_exact)).astype(jnp.int32)
    large = jnp.minimum(large, N_BUCKETS - 1)
    return jnp.where(dist < max_exact, dist, large)


def conformer_conv(u, dw_kernel, dw_bias, ln_g, ln_b, w_proj):
    a, g = jnp.split(u, 2, axis=-1)
    z = a * jax.nn.sigmoid(g)
    z = lax.conv_general_dilated(
        z, dw_kernel[:, None, :].astype(z.dtype),
        window_strides=(1,), padding=((CONV_WIDTH - 1, 0),),
        dimension_numbers=("NWC", "WIO", "NWC"),
        feature_group_count=CONV_CH) + dw_bias
    z = jax.nn.silu(layernorm(z, ln_g, ln_b))
    return z @ w_proj


def sliding_window_gqa(q, k, v, q_norm_g, k_norm_g, sinks, rel_bias, w_o):
    B, S = q.shape[0], q.shape[1]
    nb = S // BLOCK
    q = rmsnorm(q, q_norm_g)
    k = rmsnorm(k, k_norm_g)
    qb = q.reshape(B, nb, BLOCK, N_KV_HEADS, GROUP, HEAD_DIM)
    kb = k.reshape(B, nb, BLOCK, N_KV_HEADS, HEAD_DIM)
    vb = v.reshape(B, nb, BLOCK, N_KV_HEADS, HEAD_DIM)
    kpad = jnp.zeros_like(kb[:, :1])
    vpad = jnp.zeros_like(vb[:, :1])
    kw = jnp.concatenate([jnp.concatenate([kpad, kb[:, :-1]], 1), kb], axis=2)
    vw = jnp.concatenate([jnp.concatenate([vpad, vb[:, :-1]], 1), vb], axis=2)

    s = jnp.einsum("bnqhgd,bnkhd->bnhgqk", qb, kw).astype(jnp.float32)
    s = s * (1.0 / math.sqrt(HEAD_DIM))

    qi = jnp.arange(BLOCK, dtype=jnp.int32)[:, None]
    kj = jnp.arange(2 * BLOCK, dtype=jnp.int32)[None, :]
    dist = qi + BLOCK - kj
    in_win = (dist >= 0) & (dist < WINDOW)
    bucket = t5_causal_bucket(jnp.maximum(dist, 0))
    bias = rel_bias.astype(jnp.float32)[bucket]
    bias = jnp.transpose(bias, (2, 0, 1)).reshape(N_KV_HEADS, GROUP, BLOCK, 2 * BLOCK)
    key_pos = jnp.arange(nb, dtype=jnp.int32)[:, None] * BLOCK - BLOCK + kj
    valid = in_win[None] & (key_pos >= 0)[:, None, :]

    s = s + bias[None, None]
    s = jnp.where(valid[None, :, None, None], s, jnp.finfo(jnp.float32).min)
    sink = sinks.astype(jnp.float32).reshape(N_KV_HEADS, GROUP)[None, None, :, :, None, None]
    m = jnp.maximum(jnp.max(s, axis=-1, keepdims=True), sink)
    p = jnp.exp(s - m)
    p = p / (jnp.sum(p, axis=-1, keepdims=True) + jnp.exp(sink - m))
    o = jnp.einsum("bnhgqk,bnkhd->bnqhgd", p.astype(vw.dtype), vw)
    return o.reshape(B, S, Q_W) @ w_o


def setup_inputs(seed: int = 0) -> dict:
    key = jax.random.key(seed)
    ks = jax.random.split(key, 24)
    f32 = jnp.float32

    def w(k, shape, fan_in):
        return jax.random.normal(k, shape, f32) * (fan_in ** -0.5)

    def gain(k, n):
        return jnp.ones((n,), f32) + 0.01 * jax.random.normal(k, (n,), f32)

    return {
        "x": jax.random.normal(ks[0], (BATCH, SEQ, D_MODEL), f32),
        "ffn1_norm": gain(ks[1], D_MODEL),
        "ffn1_w_in": w(ks[2], (D_MODEL, 2 * D_FF), D_MODEL),
        "ffn1_w_out": w(ks[3], (D_FF, D_MODEL), D_FF),
        "mix_norm": gain(ks[4], D_MODEL),
        "w_in": w(ks[5], (D_MODEL, IN_W), D_MODEL),
        "conv_dw_kernel": w(ks[6], (CONV_WIDTH, CONV_CH), CONV_WIDTH),
        "conv_dw_bias": 0.01 * jax.random.normal(ks[7], (CONV_CH,), f32),
        "conv_ln_g": gain(ks[8], CONV_CH),
        "conv_ln_b": 0.01 * jax.random.normal(ks[9], (CONV_CH,), f32),
        "conv_w_proj": w(ks[10], (CONV_CH, D_MODEL), CONV_CH),
        "q_norm": gain(ks[11], HEAD_DIM),
        "k_norm": gain(ks[12], HEAD_DIM),
        "attn_sinks": jax.random.normal(ks[13], (N_Q_HEADS,), f32),
        "rel_bias": 0.1 * jax.random.normal(ks[14], (N_BUCKETS, N_Q_HEADS), f32),
        "attn_w_o": w(ks[15], (Q_W, D_MODEL), Q_W),
        "w_out": w(ks[16], (D_MODEL, D_MODEL), D_MODEL),
        "ffn2_norm": gain(ks[17], D_MODEL),
        "ffn2_w_in": w(ks[18], (D_MODEL, 2 * D_FF), D_MODEL),
        "ffn2_w_out": w(ks[19], (D_FF, D_MODEL), D_FF),
    }


def reference(x, ffn1_norm, ffn1_w_in, ffn1_w_out, mix_norm, w_in,
              conv_dw_kernel, conv_dw_bias, conv_ln_g, conv_ln_b, conv_w_proj,
              q_norm, k_norm, attn_sinks, rel_bias, attn_w_o, w_out,
              ffn2_norm, ffn2_w_in, ffn2_w_out):
    B, S = x.shape[0], x.shape[1]
    for _ in range(DEPTH):
        x = x + 0.5 * swiglu_ffn(rmsnorm(x, ffn1_norm), ffn1_w_in, ffn1_w_out)

        h = rmsnorm(x, mix_norm)
        idx = np.cumsum(SPLITS)[:-1].tolist()
        u_conv, q, k, v, g_conv, g_attn = jnp.split(h @ w_in, idx, axis=-1)
        a = conformer_conv(u_conv, conv_dw_kernel, conv_dw_bias, conv_ln_g, conv_ln_b, conv_w_proj)
        b = sliding_window_gqa(q.reshape(B, S, N_Q_HEADS, HEAD_DIM),
                               k.reshape(B, S, N_KV_HEADS, HEAD_DIM),
                               v.reshape(B, S, N_KV_HEADS, HEAD_DIM),
                               q_norm, k_norm, attn_sinks, rel_bias, attn_w_o)
        merged = jax.nn.sigmoid(g_conv) * a + jax.nn.sigmoid(g_attn) * b
        x = x + merged @ w_out

        x = x + 0.5 * swiglu_ffn(rmsnorm(x, ffn2_norm), ffn2_w_in, ffn2_w_out)
    return x
```

```python
import math
from contextlib import ExitStack

import numpy as np
import concourse.bass as bass
import concourse.mybir as mybir
from concourse.bass_utils import run_bass_kernel_spmd

F32 = mybir.dt.float32
BF16 = mybir.dt.bfloat16
AF = mybir.ActivationFunctionType
ALU = mybir.AluOpType

S = 4096
D = 1024
DFF = 2816
T = 512
TB = T // 128
KC = D // 128
FC = DFF // 128
EPS = 1e-6
CW = 31
R = 6
PF = 2
NSLAB = 60
NPREP = 60
SLAB_E = 4096

F1_UP, F1_DN = 0, 11
MX_CV, MX_Q, MX_K, MX_V = 17, 21, 23, 24
MX_MG, MX_WO, MX_DG = 25, 33, 35
F2_UP, F2_DN = 43, 54


GPERM = [0, 2, 1, 3]


def head_pair(kc):
    if kc < 4:
        return GPERM[kc], 4 + GPERM[kc]
    return 8 + GPERM[kc - 4], 12 + GPERM[kc - 4]


class KB:
    ENG = ["pe", "act", "dve", "pool", "sp"]

    def __init__(self):
        self.ops = {e: [] for e in self.ENG}
        self.seq = {e: 0 for e in self.ENG}
        self.last_w = {}
        self.readers = {}
        self.seen = {e: {} for e in self.ENG}
        self.dma_cnt = {}

    def rec(self, eng, fn, reads=(), writes=(), dma=None):
        waits = {}

        def need(dep, raw):
            semkey, val, deng = dep
            if deng == eng and not raw:
                return
            if deng == "dma":
                val = self.dma_cnt[semkey]
            if self.seen[eng].get(semkey, 0) >= val:
                return
            if waits.get(semkey, 0) < val:
                waits[semkey] = val

        for t in reads:
            if t in self.last_w:
                need(self.last_w[t], True)
        for t in writes:
            if t in self.last_w:
                need(self.last_w[t], False)
            for k, (v, de) in self.readers.get(t, {}).items():
                need((k, v, de), False)
        for k, v in waits.items():
            self.seen[eng][k] = v
        if fn is None:
            self.ops[eng].append((list(waits.items()), None, None))
            return
        if dma is None:
            self.seq[eng] += 1
            done = (eng, self.seq[eng], eng)
        else:
            self.dma_cnt[dma] = self.dma_cnt.get(dma, 0) + 16
            done = (dma, self.dma_cnt[dma], "dma")
        self.ops[eng].append((list(waits.items()), fn, done))
        for t in reads:
            self.readers.setdefault(t, {})[done[0]] = (done[1], done[2])
        for t in writes:
            self.last_w[t] = done
            self.readers[t] = {}


def build(ntiles=S // T, stop=3, mstop=9):
    nc = bass.Bass("TRN2", target_bir_lowering=False)
    dt = nc.dram_tensor
    x_d = dt("x", [S, D], F32, kind="ExternalInput").ap()
    f1n_d = dt("ffn1_norm", [D], F32, kind="ExternalInput").ap()
    f1wi_d = dt("ffn1_w_in", [D, 2 * DFF], F32, kind="ExternalInput").ap()
    f1wo_d = dt("ffn1_w_out", [DFF, D], F32, kind="ExternalInput").ap()
    mxn_d = dt("mix_norm", [D], F32, kind="ExternalInput").ap()
    win_d = dt("w_in", [D, 5632], F32, kind="ExternalInput").ap()
    cdk_d = dt("conv_dw_kernel", [CW, D], F32, kind="ExternalInput").ap()
    cdb_d = dt("conv_dw_bias", [D], F32, kind="ExternalInput").ap()
    clg_d = dt("conv_ln_g", [D], F32, kind="ExternalInput").ap()
    clb_d = dt("conv_ln_b", [D], F32, kind="ExternalInput").ap()
    cwp_d = dt("conv_w_proj", [D, D], F32, kind="ExternalInput").ap()
    qn_d = dt("q_norm", [64], F32, kind="ExternalInput").ap()
    kn_d = dt("k_norm", [64], F32, kind="ExternalInput").ap()
    snk_d = dt("attn_sinks", [16], F32, kind="ExternalInput").ap()
    rb_d = dt("rel_bias", [32, 16], F32, kind="ExternalInput").ap()
    awo_d = dt("attn_w_o", [D, D], F32, kind="ExternalInput").ap()
    wout_d = dt("w_out", [D, D], F32, kind="ExternalInput").ap()
    f2n_d = dt("ffn2_norm", [D], F32, kind="ExternalInput").ap()
    f2wi_d = dt("ffn2_w_in", [D, 2 * DFF], F32, kind="ExternalInput").ap()
    f2wo_d = dt("ffn2_w_out", [DFF, D], F32, kind="ExternalInput").ap()
    ident_d = dt("c_ident", [128, 128], F32, kind="ExternalInput").ap()
    oh_d = dt("c_onehot", [32, 128], F32, kind="ExternalInput").ap()
    out_d = dt("out", [S, D], F32, kind="ExternalOutput").ap()
    wscr = dt("wscr", [NSLAB, 128, SLAB_E], BF16).ap()
    a_d = dt("a_scr", [16, 383], F32).ap()
    rbb_h = dt("rb_scr", [16 * 128 * 383], F32)
    rbb = rbb_h.ap()

    kb = KB()
    es = ExitStack()
    sb = lambda name, shape, dtype: es.enter_context(nc.sbuf_tensor(name, shape, dtype))
    with es:
        x_bufs = [sb("x_a", [128, TB, D], F32), sb("x_b", [128, TB, D], F32)]
        h_fm = sb("h_fm", [128, KC, T], BF16)
        hid = sb("hid", [128, FC, T], BF16)
        zb_ = sb("z", [128, KC, 32 + T], BF16)
        qn = sb("qn", [128, KC, T], BF16)
        kn = sb("kn", [128, 4, 128 + T], BF16)
        v_aug = sb("v_aug", [128, TB + 1, 4, 128], BF16)
        cact = sb("cact", [128, KC, T], BF16)
        o_fm = sb("o_fm", [128, KC, T], BF16)
        EB = sb("EB", [128, 4, 2, 512], F32)
        esk = sb("esk", [128, 16], F32)
        wring = sb("wring", [128, R, SLAB_E], BF16)
        NTF, NTB = 6, 6
        tf = sb("tf", [128, NTF, 512], F32)
        tbf = sb("tbf", [128, NTB, 512], BF16)
        ptile = sb("ptile", [128, 4, 512], BF16)
        mgh = sb("mgh", [128, 4, 512], F32)
        lnm = sb("lnm", [128, 512], F32)
        lnr = sb("lnr", [128, 512], F32)
        ident_f = sb("ident_f", [128, 128], F32)
        ident_b = sb("ident_b", [128, 128], BF16)
        ones_b = sb("ones_b", [128, 128], BF16)
        bones_b = sb("bones_b", [128, 128], BF16)
        pvec = sb("pvec", [128, KC, 40], F32)
        gk8 = sb("gk8", [128, 1], F32)
        ss = sb("ss", [128, TB], F32)
        rstd = sb("rstd", [128, TB], F32)
        ssq = sb("ssq", [128, TB], F32)
        oh_sb = sb("oh_sb", [32, 128], F32)
        rb_sb = sb("rb_sb", [32, 16], F32)
        arow = sb("arow", [16, 383], F32)
        ps = es.enter_context(nc.psum_tensor("ps", [128, 8, 512], F32))
        rows = cact[0:40, 0:4, :].rearrange("p a n -> p (a n)").bitcast(F32)
        junk = ptile[:, 0:2, :].rearrange("p a n -> p (a n)")
        xn_v = lambda tb: o_fm[:, 2 * tb:2 * tb + 2, :].rearrange("p a n -> p (a n)")
        XNT = lambda tb: [("o", 2 * tb), ("o", 2 * tb + 1)]
        cur = {"x": x_bufs[0], "p": 0}
        XT = lambda tb, hf: ("x", cur["p"], tb, hf)

        sems = {}
        for k in ["pe", "act", "dve", "pool", "xl", "xs", "misc", "cst"]:
            sems[k] = es.enter_context(nc.semaphore(k))
        for i in range(R):
            sems["w%d" % i] = es.enter_context(nc.semaphore("w%d" % i))
        for i in range(NPREP):
            sems["p%d" % i] = es.enter_context(nc.semaphore("p%d" % i))

        P_BIAS, P_LNG, P_LNB, P_F1, P_MX, P_F2, P_QK = 31, 32, 33, 34, 35, 36, 37

        state = {"bank": 0, "tf": 0, "tb": 0}
        reserved = set()

        def bank():
            for _ in range(16):
                b = state["bank"]
                state["bank"] = (b + 1) % 8
                if b not in reserved:
                    return b
            raise RuntimeError("no bank")

        def tfs():
            s_ = state["tf"]
            state["tf"] = (s_ + 1) % NTF
            return s_

        def tbs():
            s_ = state["tb"]
            state["tb"] = (s_ + 1) % NTB
            return s_

        def psb(b):
            return ps[:, b, 0:256].bitcast(BF16)

        def mm(reads, writes, mms):
            def fn(e):
                inst = None
                for (o, l, r, st, sp) in mms:
                    inst = e.matmul(o, l, r, start=st, stop=sp)
                return inst
            kb.rec("pe", fn, reads, writes)

        def op(eng, reads, writes, f):
            kb.rec(eng, f, reads, writes)

        def dma(eng, semkey, reads, writes, out, in_, **kw):
            kb.rec(eng, lambda e: e.dma_start(out=out, in_=in_, **kw), reads, writes, dma=semkey)

        dma("sp", "xl", [], [("x", 0, tb, hf) for tb in range(TB) for hf in range(2)], x_bufs[0][:],
            x_d[0:T, :].rearrange("(tb p) d -> p tb d", p=128))
        dma("sp", "cst", [], [("ident_f",)], ident_f[:], ident_d)
        dma("sp", "cst", [], [("oh",)], oh_sb[:], oh_d)
        dma("sp", "cst", [], [("rb",)], rb_sb[:], rb_d)
        dma("sp", "cst", [], [("esk",)], esk[:], snk_d.partition_broadcast(128))
        op("pool", [], [("rows",)], lambda e: e.memset(rows, 0.0))
        dma("sp", "cst", [("rows",)], [("rows", 0)], rows[0:CW, :], cdk_d)
        for r_, v_ in [(P_BIAS, cdb_d), (P_LNG, clg_d), (P_LNB, clb_d), (P_F1, f1n_d), (P_MX, mxn_d), (P_F2, f2n_d)]:
            dma("sp", "cst", [("rows",)], [("rows", r_)], rows[r_:r_ + 1, :], v_.rearrange("(o n) -> o n", o=1))
        for i_, v_ in enumerate([qn_d, qn_d, kn_d, kn_d]):
            dma("sp", "cst", [("rows",)], [("rows", 100 + i_)], rows[P_QK:P_QK + 1, 64 * i_:64 * i_ + 64],
                v_.rearrange("(o n) -> o n", o=1))
        ROWTOK = [("rows", 0)] + [("rows", r_) for r_ in range(P_BIAS, P_F2 + 1)] + [("rows", 100 + i_) for i_ in range(4)]
        op("dve", [("ident_f",)], [("ident_b",)], lambda e: e.tensor_copy(out=ident_b[:], in_=ident_f[:]))
        op("pool", [], [("ones_b",)], lambda e: e.memset(ones_b[:], 1.0))
        op("pool", [], [("bones_b",)], lambda e: e.memset(bones_b[:], 0.0))
        op("pool", [("bones_b",)], [("bones_b",)], lambda e: e.memset(bones_b[0:64, 0:64], 1.0))
        op("pool", [("bones_b",)], [("bones_b",)], lambda e: e.memset(bones_b[64:128, 64:128], 1.0))
        for c in range(KC):
            b = bank()
            op("pe", ROWTOK + [("ident_f",)], [("ps", b)],
               lambda e, c=c, b=b: e.transpose(ps[:, b, 0:40], rows[0:40, c * 128:(c + 1) * 128], ident_f[0:40, 0:40]))
            op("dve", [("ps", b)], [("pvec",)], lambda e, c=c, b=b: e.tensor_copy(out=pvec[:, c, :], in_=ps[:, b, 0:40]))
        op("dve", [("pvec",)], [("gk8",)],
           lambda e: e.tensor_scalar(out=gk8[:], in0=pvec[:, 1, P_QK:P_QK + 1], scalar1=8.0, scalar2=None, op0=ALU.mult))
        op("act", [("esk",)], [("esk",)], lambda e: e.activation(out=esk[:], in_=esk[:], func=AF.Exp))
        op("pool", [], [("v", i) for i in range(TB + 1)], lambda e: e.memset(v_aug[:], 1.0))
        op("pool", [], [("zh", c) for c in range(KC)], lambda e: e.memset(zb_[:, :, 0:32], 0.0))
        op("pool", [], [("kn", h, 0) for h in range(4)], lambda e: e.memset(kn[:, :, 0:128], 0.0))
        b = bank()
        op("pe", [("oh",), ("rb",)], [("ps", b)],
           lambda e, b=b: e.matmul(ps[0:16, b, 0:128], rb_sb[:, :], oh_sb[:, :], start=True, stop=True))
        op("pool", [], [("arow",)], lambda e: e.memset(arow[:], 0.0))
        op("act", [("ps", b), ("arow",)], [("arow",)],
           lambda e, b=b: e.activation(out=arow[:, 127:255], in_=ps[0:16, b, 0:128], func=AF.Exp))
        EBTOK = lambda h, xy: [("EB", h, xy, role, gi) for role in range(2) for gi in range(2)]

        prep_n = [0]

        def prep(slab, out, in_, extra_reads=(), eng="pool"):
            semkey = "p%d" % (slab % NPREP)
            dma(eng, semkey, list(extra_reads), [("scr", slab, prep_n[0])], out, in_)
            scr_tok.setdefault(slab, []).append(("scr", slab, prep_n[0]))
            prep_n[0] += 1

        scr_tok = {}
        sv = lambda s_, n: wscr[s_].rearrange("p (k n) -> p k n", n=n)
        wv_ = lambda w: w.rearrange("(k p) n -> p k n", p=128)

        def prep_ffn(up0, dn0, wi, wo, jobs=None):
            wiv = wv_(wi)
            lst = []
            for s_ in range(11):
                lst.append((up0 + s_, sv(up0 + s_, 512)[:, 0:KC, 0:256], wiv[:, :, 256 * s_:256 * s_ + 256]))
                lst.append((up0 + s_, sv(up0 + s_, 512)[:, 0:KC, 256:512], wiv[:, :, DFF + 256 * s_:DFF + 256 * s_ + 256]))
            wov = wv_(wo)
            for half in range(2):
                for ks in range(3):
                    k0, k1 = 8 * ks, min(8 * ks + 8, FC)
                    lst.append((dn0 + half * 3 + ks, sv(dn0 + half * 3 + ks, 512)[:, 0:k1 - k0, :], wov[:, k0:k1, half * 512:(half + 1) * 512]))
            for a_ in lst:
                if jobs is None:
                    prep(*a_)
                else:
                    jobs.append(lambda a_=a_: prep(*a_))

        prep_ffn(F1_UP, F1_DN, f1wi_d, f1wo_d)
        dma("pool", "misc", [("arow",)], [("a_d",)], a_d, arow[:])
        dma("pool", "misc", [("a_d",)], [("rbb",)],
            bass.AP(rbb_h, 0, [[128 * 383, 16], [383, 128], [1, 383]]),
            bass.AP(a_d.tensor, 0, [[383, 16], [0, 128], [1, 383]]))
        for h in range(4):
            for xy in range(2):
                for role in range(2):
                    c0 = 255 if role == 0 else 127
                    for gi in range(2):
                        j = 4 * h + 2 * gi + xy
                        col = (role * 2 + gi) * 128
                        dma("pool", "misc", [("rbb",)], [("EB", h, xy, role, gi)], EB[:, h, xy, col:col + 128],
                            bass.AP(rbb_h, j * 128 * 383 + c0, [[382, 128], [1, 128]]))

        kb.rec("pool", None, reads=[tk for s_ in range(F1_UP, F1_UP + 17) for tk in scr_tok[s_]])
        winv = wv_(win_d)
        for s_ in range(4):
            prep(MX_CV + s_, sv(MX_CV + s_, 512)[:, 0:KC, 0:256], winv[:, :, 256 * s_:256 * s_ + 256])
            prep(MX_CV + s_, sv(MX_CV + s_, 512)[:, 0:KC, 256:512], winv[:, :, 1024 + 256 * s_:1024 + 256 * s_ + 256])
        for s_ in range(2):
            prep(MX_Q + s_, sv(MX_Q + s_, 512)[:, 0:KC, :], winv[:, :, 2048 + 512 * s_:2048 + 512 * s_ + 512])
        for i_ in range(8):
            prep(MX_K, sv(MX_K, 512)[:, 0:KC, 64 * i_:64 * i_ + 64], winv[:, :, 3072 + 64 * (i_ // 2):3072 + 64 * (i_ // 2) + 64])
        prep(MX_V, sv(MX_V, 512)[:, 0:KC, 0:256], winv[:, :, 3328:3584])
        cwpv = wv_(cwp_d)
        for half in range(2):
            base = MX_MG + 4 * half
            prep(base + 0, sv(base + 0, 512)[:, 0:KC, :], winv[:, :, 3584 + 512 * half:3584 + 512 * half + 512])
            prep(base + 1, sv(base + 1, 512)[:, 0:KC, :], winv[:, :, 4608 + 512 * half:4608 + 512 * half + 512])
            prep(base + 2, sv(base + 2, 512)[:, 0:KC, :], cwpv[:, :, 512 * half:512 * half + 512])
            for kc in range(KC):
                ha, hb = head_pair(kc)
                prep(base + 3, sv(base + 3, 512)[0:64, kc, :], awo_d[64 * ha:64 * ha + 64, 512 * half:512 * half + 512])
                prep(base + 3, sv(base + 3, 512)[64:128, kc, :], awo_d[64 * hb:64 * hb + 64, 512 * half:512 * half + 512])
        woutv = wv_(wout_d)
        for half in range(2):
            prep(MX_WO + half, sv(MX_WO + half, 512)[:, 0:KC, :], woutv[:, :, 512 * half:512 * half + 512])
        def diag_jobs():
            jobs = []
            for c in range(KC):
                for g4 in range(4):
                    def job(c=c, g4=g4):
                        ts = tfs()
                        stage = tf[:, ts, :].bitcast(BF16).rearrange("p (j n) -> p j n", n=128)
                        taps = range(8 * g4, min(8 * g4 + 8, CW))

                        def fnd(e, stage=stage, taps=taps, c=c):
                            inst = None
                            for jj, j in enumerate(taps):
                                inst = e.tensor_scalar(out=stage[:, jj, :], in0=ident_b[:], scalar1=pvec[:, c, j:j + 1],
                                                       scalar2=None, op0=ALU.mult)
                            return inst
                        op("dve", [("pvec",), ("ident_b",)], [("tf", ts)], fnd)
                        n_ = len(taps)
                        prep(MX_DG + c, sv(MX_DG + c, 128)[:, 8 * g4:8 * g4 + n_, :], stage[:, 0:n_, :],
                             extra_reads=[("tf", ts)], eng="act")
                    jobs.append(job)
            return jobs

        tile_sched = (list(range(F1_UP, F1_UP + 17)) + list(range(MX_CV, MX_CV + 4)) + [MX_Q, MX_Q + 1, MX_K, MX_V]
                      + list(range(MX_DG, MX_DG + 8)) + list(range(MX_MG, MX_MG + 8)) + [MX_WO, MX_WO + 1]
                      + list(range(F2_UP, F2_UP + 17)))
        if stop == 1:
            tile_sched = tile_sched[:17]
        elif stop == 2:
            tile_sched = tile_sched[:17 + 26]
        elif stop == 0:
            tile_sched = []
        sched = tile_sched * ntiles
        slab_ne = {}
        for s_ in range(NSLAB):
            slab_ne[s_] = SLAB_E
        for base in (F1_DN, F2_DN):
            for half in range(2):
                slab_ne[base + half * 3 + 2] = 6 * 512
        slab_ne[MX_V] = SLAB_E
        ring = {"pos": 0, "loaded": 0, "done": -1}
        PFMAX = 4

        def wget(slab, keep=False):
            pos = ring["pos"]
            assert sched[pos] == slab, (pos, sched[pos], slab)
            ring["pos"] += 1
            if not keep:
                ring["done"] = pos - 1
            while (ring["loaded"] <= min(pos + PFMAX, len(sched) - 1)
                   and (ring["loaded"] - R <= ring["done"] or ring["loaded"] <= pos)):
                i = ring["loaded"]
                assert i - R <= ring["done"], "weight ring too small"
                sl = i % R
                s2 = sched[i]
                ne = slab_ne[s2]
                pk = "p%d" % (s2 % NPREP)
                kb.last_w[("scrall", s2)] = (pk, kb.dma_cnt[pk], "dma")
                dma("sp", "w%d" % sl, [("scrall", s2)], [("w", sl)], wring[:, sl, 0:ne], wscr[s2, :, 0:ne])
                ring["loaded"] += 1
            return pos % R

        HTOK = [("h", c) for c in range(KC)]

        def rmsnorm(prow):
            x_tm = cur["x"]
            for tb in range(TB):
                op("act", [XT(tb, 0), XT(tb, 1)], [("pt", 0), ("pt", 1), ("ss", tb)],
                   lambda e, tb=tb: e.activation(out=junk, in_=x_tm[:, tb, :], func=AF.Square, accum_out=ss[:, tb:tb + 1]))
                op("act", [("ss", tb)], [("ssq", tb)],
                   lambda e, tb=tb: e.activation(out=ssq[:, tb:tb + 1], in_=ss[:, tb:tb + 1], func=AF.Ln, bias=float(D * EPS)))
                op("act", [("ssq", tb)], [("rstd", tb)],
                   lambda e, tb=tb: e.activation(out=rstd[:, tb:tb + 1], in_=ssq[:, tb:tb + 1], func=AF.Exp, scale=-0.5))
                op("dve", [XT(tb, 0), XT(tb, 1), ("rstd", tb)], XNT(tb),
                   lambda e, tb=tb: e.tensor_scalar(out=xn_v(tb), in0=x_tm[:, tb, :], scalar1=rstd[:, tb:tb + 1],
                                                    scalar2=None, op0=ALU.mult))
            for c in range(KC):
                b = bank()

                def fn(e, c=c, b=b):
                    inst = None
                    for tb in range(TB):
                        inst = e.transpose(psb(b)[:, tb * 128:(tb + 1) * 128], xn_v(tb)[:, c * 128:(c + 1) * 128], ident_b[:])
                    return inst
                op("pe", [tk for tb in range(TB) for tk in XNT(tb)] + [("ident_b",)], [("ps", b)], fn)
                op("dve", [("ps", b), ("pvec",)], [("h", c)],
                   lambda e, c=c, b=b: e.tensor_scalar(out=h_fm[:, c, :], in0=psb(b), scalar1=pvec[:, c, prow:prow + 1],
                                                       scalar2=float(math.sqrt(D)), op0=ALU.mult, op1=ALU.mult))

        def ffn(up0, dn0, extra=None):
            for s_ in range(11):
                sl = wget(up0 + s_)
                wv = wring[:, sl, :].rearrange("p (k n) -> p k n", n=512)
                for jj in range(2):
                    i = 2 * s_ + jj
                    ba, bb = bank(), bank()
                    mm([("w", sl)] + HTOK, [("ps", ba)],
                       [(ps[:, ba, :], wv[:, kc, jj * 128:(jj + 1) * 128], h_fm[:, kc, :], kc == 0, kc == KC - 1) for kc in range(KC)])
                    mm([("w", sl)] + HTOK, [("ps", bb)],
                       [(ps[:, bb, :], wv[:, kc, 256 + jj * 128:256 + (jj + 1) * 128], h_fm[:, kc, :], kc == 0, kc == KC - 1) for kc in range(KC)])
                    ts = tfs()
                    op("act", [("ps", ba)], [("tf", ts)],
                       lambda e, ba=ba, ts=ts: e.activation(out=tf[:, ts, :], in_=ps[:, ba, :], func=AF.Silu))
                    op("dve", [("ps", bb), ("tf", ts)], [("hid", i)],
                       lambda e, bb=bb, ts=ts, i=i: e.tensor_tensor(out=hid[:, i, :], in0=ps[:, bb, :], in1=tf[:, ts, :], op=ALU.mult))
                    if extra:
                        extra.pop(0)()
            while extra:
                extra.pop(0)()
            for half in range(2):
                banks = []
                for tb in range(TB):
                    b = bank()
                    reserved.add(b)
                    banks.append(b)
                for ks in range(3):
                    sl = wget(dn0 + half * 3 + ks)
                    wv = wring[:, sl, :].rearrange("p (k n) -> p k n", n=512)
                    k0, k1 = 8 * ks, min(8 * ks + 8, FC)
                    for tb in range(TB):
                        mm([("w", sl)] + [("hid", kc) for kc in range(k0, k1)], [("ps", banks[tb])],
                           [(ps[:, banks[tb], :], hid[:, kc, tb * 128:(tb + 1) * 128], wv[:, kc - k0, :], kc == 0, kc == FC - 1)
                            for kc in range(k0, k1)])
                for tb in range(TB):
                    b = banks[tb]
                    xs_ = cur["x"][:, tb, half * 512:(half + 1) * 512]
                    op("dve", [("ps", b), XT(tb, half)], [XT(tb, half)],
                       lambda e, b=b, xs_=xs_: e.scalar_tensor_tensor(out=xs_, in0=ps[:, b, :], scalar=0.5, in1=xs_,
                                                                      op0=ALU.mult, op1=ALU.add))
                    reserved.discard(b)

        def qk_stage_a(wv, col0):
            bq = bank()
            mm([("w", wv[1])] + HTOK, [("ps", bq)],
               [(ps[:, bq, :], wv[0][:, kc, col0:col0 + 128], h_fm[:, kc, :], kc == 0, kc == KC - 1) for kc in range(KC)])
            t2 = tbs()
            op("act", [("ps", bq)], [("tb", t2)], lambda e: e.activation(out=tbf[:, t2, :], in_=ps[:, bq, :], func=AF.Square))
            return bq, t2

        def qk_stage_b(st, gain_ap, out_ap, out_tok):
            bq, t2 = st
            bs = bank()
            mm([("tb", t2), ("bones_b",)], [("ps", bs)], [(ps[:, bs, :], bones_b[:], tbf[:, t2, :], True, True)])
            t3, t3b = tfs(), tfs()
            op("act", [("ps", bs)], [("tf", t3b)],
               lambda e: e.activation(out=tf[:, t3b, :], in_=ps[:, bs, :], func=AF.Ln, bias=float(64 * EPS)))
            op("act", [("tf", t3b)], [("tf", t3)],
               lambda e: e.activation(out=tf[:, t3, :], in_=tf[:, t3b, :], func=AF.Exp, scale=-0.5))
            op("dve", [("ps", bq), ("tf", t3), ("pvec",), ("gk8",)], out_tok,
               lambda e: e.scalar_tensor_tensor(out=out_ap, in0=ps[:, bq, :], scalar=gain_ap, in1=tf[:, t3, :],
                                                op0=ALU.mult, op1=ALU.mult))

        def mixer(t):
            for s_ in range(4):
                sl = wget(MX_CV + s_)
                wv = wring[:, sl, :].rearrange("p (k n) -> p k n", n=512)
                for jj in range(2):
                    c = 2 * s_ + jj
                    bg, ba = bank(), bank()
                    mm([("w", sl)] + HTOK, [("ps", bg)],
                       [(ps[:, bg, :], wv[:, kc, 256 + jj * 128:256 + (jj + 1) * 128], h_fm[:, kc, :], kc == 0, kc == KC - 1) for kc in range(KC)])
                    mm([("w", sl)] + HTOK, [("ps", ba)],
                       [(ps[:, ba, :], wv[:, kc, jj * 128:(jj + 1) * 128], h_fm[:, kc, :], kc == 0, kc == KC - 1) for kc in range(KC)])
                    ts = tfs()
                    op("act", [("ps", bg)], [("tf", ts)],
                       lambda e, bg=bg, ts=ts: e.activation(out=tf[:, ts, :], in_=ps[:, bg, :], func=AF.Sigmoid))
                    op("dve", [("ps", ba), ("tf", ts)], [("z", c)],
                       lambda e, ba=ba, ts=ts, c=c: e.tensor_tensor(out=zb_[:, c, 32:32 + T], in0=ps[:, ba, :], in1=tf[:, ts, :], op=ALU.mult))
            if mstop <= 1:
                return
            jobs = []
            for s_ in range(2):
                for jj in range(4):
                    c = 4 * s_ + jj
                    jobs.append((MX_Q + s_, jj * 128, pvec[:, 0, P_QK:P_QK + 1], qn[:, c, :], [("qn", c)]))
            for h in range(4):
                jobs.append((MX_K, h * 128, gk8[:, 0:1], kn[:, h, 128:128 + T], [("kn", h, 1 + tb) for tb in range(TB)]))
            cur_slab, sl, wv = None, None, None
            pend = []
            for (slab, col0, gain_ap, out_ap, out_tok) in jobs:
                if slab != cur_slab:
                    sl = wget(slab)
                    wv = wring[:, sl, :].rearrange("p (k n) -> p k n", n=512)
                    cur_slab = slab
                st = qk_stage_a((wv, sl), col0)
                pend.append((st, gain_ap, out_ap, out_tok))
                if len(pend) > 1:
                    qk_stage_b(*pend.pop(0))
            while pend:
                qk_stage_b(*pend.pop(0))
            if mstop <= 3:
                return
            sl = wget(MX_V)
            wv = wring[:, sl, :].rearrange("p (k n) -> p k n", n=512)
            for tb in range(TB):
                b = bank()
                mm([("w", sl)] + HTOK, [("ps", b)],
                   [(ps[:, b, 0:256], h_fm[:, kc, tb * 128:(tb + 1) * 128], wv[:, kc, 0:256], kc == 0, kc == KC - 1) for kc in range(KC)])
                def fnv(e, b=b, tb=tb):
                    inst = None
                    for h in range(4):
                        off = 0 if h % 2 == 0 else 64
                        inst = e.activation(out=v_aug[:, 1 + tb, h, off:off + 64], in_=ps[:, b, 64 * h:64 * h + 64], func=AF.Identity)
                    return inst
                op("act", [("ps", b)], [("v", 1 + tb)], fnv)
            b1 = bank()
            reserved.add(b1)
            b2 = bank()
            reserved.add(b2)

            def conv_a(c):
                sl = wget(MX_DG + c)
                dv = wring[:, sl, :].rearrange("p (j n) -> p j n", n=128)
                b = bank()
                mm([("w", sl), ("z", c), ("zh", c)], [("ps", b)],
                   [(ps[:, b, :], dv[:, j, :], zb_[:, c, 2 + j:2 + j + T], j == 0, j == CW - 1) for j in range(CW)])
                zc = hid[:, 2 * c:2 * c + 2, :].rearrange("p a n -> p (a n)").bitcast(F32)
                t1, t2 = tbs(), tbs()
                bias_ap = pvec[:, c, P_BIAS:P_BIAS + 1]
                op("act", [("ps", b), ("pvec",)], [("hid", 2 * c), ("hid", 2 * c + 1)],
                   lambda e: e.activation(out=zc, in_=ps[:, b, :], func=AF.Identity, bias=bias_ap))
                op("act", [("ps", b), ("pvec",)], [("tb", t1)],
                   lambda e: e.activation(out=tbf[:, t1, :], in_=ps[:, b, :], func=AF.Identity, bias=bias_ap))
                op("act", [("ps", b), ("pvec",)], [("tb", t2)],
                   lambda e: e.activation(out=tbf[:, t2, :], in_=ps[:, b, :], func=AF.Square, bias=bias_ap))
                return t1, t2

            def conv_b(c, t1, t2):
                mm([("tb", t1), ("ones_b",)], [("ps", b1)], [(ps[:, b1, :], ones_b[:], tbf[:, t1, :], c == 0, c == KC - 1)])
                mm([("tb", t2), ("ones_b",)], [("ps", b2)], [(ps[:, b2, :], ones_b[:], tbf[:, t2, :], c == 0, c == KC - 1)])

            def att_a(n, h, slot0):
                gblk = t * TB + n
                roles = [1] if gblk == 0 else [0, 1]
                lo = 256 if gblk == 0 else 0
                bxy = [bank(), bank()]
                mms = []
                for role in roles:
                    kcol = (n + role) * 128
                    for gi in range(2):
                        for xy in range(2):
                            j = 4 * h + 2 * gi + xy
                            cq, pb = j // 2, 64 * xy
                            col = (role * 2 + gi) * 128
                            mms.append((ps[:, bxy[xy], col:col + 128], kn[pb:pb + 64, h, kcol:kcol + 128],
                                        qn[pb:pb + 64, cq, n * 128:(n + 1) * 128], True, True))
                mm([("kn", h, n + r_) for r_ in roles] + [("qn", 2 * h), ("qn", 2 * h + 1)], [("ps", bxy[0]), ("ps", bxy[1])], mms)
                pts = []
                for xy in range(2):
                    t1 = tbs()
                    t2 = slot0 + xy
                    op("act", [("ps", bxy[xy])], [("tb", t1)],
                       lambda e, b=bxy[xy], t1=t1: e.activation(out=tbf[:, t1, lo:512], in_=ps[:, b, lo:512], func=AF.Exp))
                    op("pool", [("tb", t1)] + EBTOK(h, xy), [("pt", t2)],
                       lambda e, t1=t1, t2=t2, xy=xy: e.tensor_tensor(out=ptile[:, t2, lo:512], in0=tbf[:, t1, lo:512],
                                                                      in1=EB[:, h, xy, lo:512], op=ALU.mult))
                    pts.append(t2)
                return roles, pts

            def att_b(n, h, roles, pts):
                bo = bank()
                mms = []
                for xy in range(2):
                    for r_ in roles:
                        mms.append((ps[:, bo, xy * 256:(xy + 1) * 256], v_aug[:, n + r_, h, :],
                                    ptile[:, pts[xy], r_ * 256:(r_ + 1) * 256], r_ == roles[0], r_ == roles[-1]))
                mm([("pt", pts[0]), ("pt", pts[1])] + [("v", n + r_) for r_ in roles], [("ps", bo)], mms)
                ob = 0 if h % 2 == 0 else 64
                db = 64 - ob
                t3, t4 = tfs(), tfs()

                def fna(e):
                    inst = None
                    for i in range(4):
                        j = 4 * h + 2 * (i % 2) + (i // 2)
                        inst = e.activation(out=tf[ob:ob + 64, t3, i * 128:(i + 1) * 128],
                                            in_=ps[db:db + 64, bo, i * 128:(i + 1) * 128],
                                            func=AF.Ln, bias=esk[db:db + 64, j:j + 1])
                    return inst
                op("act", [("ps", bo), ("esk",)], [("tf", t3)], fna)
                op("act", [("tf", t3)], [("tf", t4)],
                   lambda e: e.activation(out=tf[ob:ob + 64, t4, :], in_=tf[ob:ob + 64, t3, :], func=AF.Exp, scale=-1.0))
                c0 = (0 if h < 2 else 4)
                op("dve", [("ps", bo), ("tf", t4)], [("o", c0 + g) for g in range(4)],
                   lambda e: e.tensor_tensor(
                       out=o_fm[ob:ob + 64, c0:c0 + 4, n * 128:(n + 1) * 128],
                       in0=ps[ob:ob + 64, bo, :].rearrange("p (g q) -> p g q", q=128),
                       in1=tf[ob:ob + 64, t4, :].rearrange("p (g q) -> p g q", q=128), op=ALU.mult))

            items = [(n, h) for n in range(TB) for h in range(4)]
            f2jobs = []
            if t == 0 and stop >= 3:
                prep_ffn(F2_UP, F2_DN, f2wi_d, f2wo_d, jobs=f2jobs)
            for c in range(KC):
                for _ in range(4):
                    if f2jobs:
                        f2jobs.pop(0)()
                ia, ib = items[2 * c], items[2 * c + 1]
                sa = att_a(ia[0], ia[1], 0)
                sb_ = att_a(ib[0], ib[1], 2)
                ct = conv_a(c)
                att_b(ia[0], ia[1], *sa)
                att_b(ib[0], ib[1], *sb_)
                conv_b(c, *ct)
            t3, t4 = tfs(), tfs()
            op("dve", [("ps", b1)], [("lnm",)],
               lambda e: e.tensor_scalar(out=lnm[:], in0=ps[:, b1, :], scalar1=1.0 / D, scalar2=None, op0=ALU.mult))
            op("dve", [("lnm",)], [("tf", t3)], lambda e: e.tensor_tensor(out=tf[:, t3, :], in0=lnm[:], in1=lnm[:], op=ALU.mult))
            op("dve", [("ps", b2), ("tf", t3)], [("tf", t4)],
               lambda e: e.scalar_tensor_tensor(out=tf[:, t4, :], in0=ps[:, b2, :], scalar=1.0 / D, in1=tf[:, t3, :],
                                                op0=ALU.mult, op1=ALU.subtract))
            t4b = tfs()
            op("act", [("tf", t4)], [("tf", t4b)],
               lambda e: e.activation(out=tf[:, t4b, :], in_=tf[:, t4, :], func=AF.Ln, bias=float(EPS)))
            op("act", [("tf", t4b)], [("lnr",)], lambda e: e.activation(out=lnr[:], in_=tf[:, t4b, :], func=AF.Exp, scale=-0.5))
            reserved.discard(b1)
            reserved.discard(b2)
            for c in range(KC):
                zc = hid[:, 2 * c:2 * c + 2, :].rearrange("p a n -> p (a n)").bitcast(F32)
                t5, t6 = tfs(), tfs()
                op("pool", [("hid", 2 * c), ("hid", 2 * c + 1), ("lnm",)], [("tf", t5)],
                   lambda e, zc=zc, t5=t5: e.tensor_tensor(out=tf[:, t5, :], in0=zc, in1=lnm[:], op=ALU.subtract))
                op("dve", [("tf", t5), ("lnr",)], [("tf", t6)],
                   lambda e, t5=t5, t6=t6: e.tensor_tensor(out=tf[:, t6, :], in0=tf[:, t5, :], in1=lnr[:], op=ALU.mult))
                op("act", [("tf", t6), ("pvec",)], [("cact", c)],
                   lambda e, c=c, t6=t6: e.activation(out=cact[:, c, :], in_=tf[:, t6, :], func=AF.Silu,
                                                      scale=pvec[:, c, P_LNG:P_LNG + 1], bias=pvec[:, c, P_LNB:P_LNB + 1]))
            if mstop <= 6:
                return
            for half in range(2):
                base = MX_MG + 4 * half
                sls = [wget(base + i, keep=(i > 0)) for i in range(4)]
                wvs = [wring[:, sl, :].rearrange("p (k n) -> p k n", n=512) for sl in sls]
                OTOK = [("o", c) for c in range(KC)]
                CTOK = [("cact", c) for c in range(KC)]

                def grp(i, f4, src, stok):
                    b = bank()
                    mm([("w", sls[i])] + stok, [("ps", b)],
                       [(ps[:, b, :], wvs[i][:, kc, f4 * 128:(f4 + 1) * 128], src[:, kc, :], kc == 0, kc == KC - 1) for kc in range(KC)])
                    return b

                def mg_x(f4):
                    p = f4 % 2
                    bgc = grp(0, f4, h_fm, HTOK)
                    bga = grp(1, f4, h_fm, HTOK)
                    bb_ = grp(3, f4, o_fm, OTOK)
                    t2 = tfs()
                    op("act", [("ps", bgc)], [("mgh", 2 * p)],
                       lambda e: e.activation(out=mgh[:, 2 * p, :], in_=ps[:, bgc, :], func=AF.Sigmoid))
                    op("act", [("ps", bga)], [("tf", t2)],
                       lambda e: e.activation(out=tf[:, t2, :], in_=ps[:, bga, :], func=AF.Sigmoid))
                    op("dve", [("ps", bb_), ("tf", t2)], [("mgh", 2 * p + 1)],
                       lambda e: e.tensor_tensor(out=mgh[:, 2 * p + 1, :], in0=ps[:, bb_, :], in1=tf[:, t2, :], op=ALU.mult))

                def mg_y(f4):
                    p = f4 % 2
                    fc = 4 * half + f4
                    ba_ = grp(2, f4, cact, CTOK)
                    t3 = tfs()
                    op("dve", [("ps", ba_), ("mgh", 2 * p)], [("tf", t3)],
                       lambda e: e.tensor_tensor(out=tf[:, t3, :], in0=ps[:, ba_, :], in1=mgh[:, 2 * p, :], op=ALU.mult))
                    op("pool", [("tf", t3), ("mgh", 2 * p + 1)], [("qn", fc)],
                       lambda e: e.tensor_tensor(out=qn[:, fc, :], in0=tf[:, t3, :], in1=mgh[:, 2 * p + 1, :], op=ALU.add))

                mg_x(0)
                mg_x(1)
                mg_y(0)
                mg_x(2)
                mg_y(1)
                mg_x(3)
                mg_y(2)
                mg_y(3)
            for half in range(2):
                sl = wget(MX_WO + half)
                wv = wring[:, sl, :].rearrange("p (k n) -> p k n", n=512)
                for tb in range(TB):
                    b = bank()
                    mm([("w", sl)] + [("qn", c) for c in range(KC)], [("ps", b)],
                       [(ps[:, b, :], qn[:, kc, tb * 128:(tb + 1) * 128], wv[:, kc, :], kc == 0, kc == KC - 1) for kc in range(KC)])
                    xs_ = cur["x"][:, tb, half * 512:(half + 1) * 512]
                    op("dve", [("ps", b), XT(tb, half)], [XT(tb, half)],
                       lambda e, b=b, xs_=xs_: e.tensor_tensor(out=xs_, in0=ps[:, b, :], in1=xs_, op=ALU.add))
            op("pool", [("kn", h, TB) for h in range(4)], [("kn", h, 0) for h in range(4)],
               lambda e: e.tensor_copy(out=kn[:, :, 0:128], in_=kn[:, :, T:T + 128]))
            op("pool", [("v", TB)], [("v", 0)], lambda e: e.tensor_copy(out=v_aug[:, 0, :, :], in_=v_aug[:, TB, :, :]))
            op("pool", [("z", c) for c in range(KC)], [("zh", c) for c in range(KC)],
               lambda e: e.tensor_copy(out=zb_[:, :, 0:32], in_=zb_[:, :, T:T + 32]))

        for t in range(ntiles):
            if stop >= 1:
                rmsnorm(P_F1)
                ffn(F1_UP, F1_DN, extra=(diag_jobs() if (t == 0 and stop >= 2) else None))
            if t + 1 < ntiles:
                nb = 1 - cur["p"]
                dma("sp", "xl", [], [("x", nb, tb, hf) for tb in range(TB) for hf in range(2)], x_bufs[nb][:],
                    x_d[(t + 1) * T:(t + 2) * T, :].rearrange("(tb p) d -> p tb d", p=128))
            if stop >= 2:
                rmsnorm(P_MX)
                mixer(t)
            if stop >= 3:
                rmsnorm(P_F2)
                ffn(F2_UP, F2_DN)
            for tb in range(TB):
                dma("sp", "xs", [XT(tb, 0), XT(tb, 1)], [("out", t, tb)],
                    out_d[t * T + tb * 128:t * T + (tb + 1) * 128, :], cur["x"][:, tb, :])
            cur["p"] = 1 - cur["p"]
            cur["x"] = x_bufs[cur["p"]]
        kb.rec("sp", None, reads=[("out", t, tb) for t in range(ntiles) for tb in range(TB)])

        with nc.Block() as block:
            def emit(eng_name):
                def body(e):
                    for waits, fn, done in kb.ops[eng_name]:
                        for k, v in waits:
                            e.wait_ge(sems[k], v)
                        if fn is None:
                            continue
                        inst = fn(e)
                        inst.then_inc(sems[done[0]], 16 if done[2] == "dma" else 1)
                return body
            block.tensor(emit("pe"))
            block.scalar(emit("act"))
            block.vector(emit("dve"))
            block.gpsimd(emit("pool"))
            block.sync(emit("sp"))
    return nc


def t5_bucket_onehot():
    oh = np.zeros((32, 128), np.float32)
    for d in range(128):
        if d < 16:
            b = d
        else:
            dd = np.float32(max(d, 1))
            val = np.log(dd / np.float32(16)) / np.float32(math.log(128 / 16)) * np.float32(16)
            b = min(16 + int(np.float32(val)), 31)
        oh[b, d] = 1.0
    return oh


_NC_CACHE = {}


def kernel(**inputs):
    ncores = 8
    if "nc" not in _NC_CACHE:
        _NC_CACHE["nc"] = build()
    nc = _NC_CACHE["nc"]
    x = np.ascontiguousarray(np.asarray(inputs["x"], dtype=np.float32))
    consts = {"c_ident": np.eye(128, dtype=np.float32), "c_onehot": t5_bucket_onehot()}
    shared = {k: np.ascontiguousarray(np.asarray(v, dtype=np.float32)) for k, v in inputs.items() if k != "x"}
    in_maps = []
    for b in range(ncores):
        m = dict(shared)
        m.update(consts)
        m["x"] = x[b]
        in_maps.append(m)
    res = run_bass_kernel_spmd(nc, in_maps, core_ids=list(range(ncores)))
    return np.stack([np.asarray(r["out"], dtype=np.float32) for r in res.results], axis=0)
```

```python
import math
from contextlib import ExitStack

import numpy as np
import concourse.bass as bass
import concourse.mybir as mybir
from concourse.bass_utils import run_bass_kernel_spmd

F32 = mybir.dt.float32
BF16 = mybir.dt.bfloat16
AF = mybir.ActivationFunctionType
ALU = mybir.AluOpType

S = 4096
D = 1024
DFF = 2816
T = 512
TB = T // 128
KC = D // 128
FC = DFF // 128
EPS = 1e-6
CW = 31
R = 6
PF = 2
NSLAB = 60
NPREP = 60
SLAB_E = 4096

F1_UP, F1_DN = 0, 11
MX_CV, MX_Q, MX_K, MX_V = 17, 21, 23, 24
MX_MG, MX_WO, MX_DG = 25, 33, 35
F2_UP, F2_DN = 43, 54


GPERM = [0, 2, 1, 3]


def head_pair(kc):
    if kc < 4:
        return GPERM[kc], 4 + GPERM[kc]
    return 8 + GPERM[kc - 4], 12 + GPERM[kc - 4]


class KB:
    ENG = ["pe", "act", "dve", "pool", "sp"]

    def __init__(self):
        self.ops = {e: [] for e in self.ENG}
        self.seq = {e: 0 for e in self.ENG}
        self.last_w = {}
        self.readers = {}
        self.seen = {e: {} for e in self.ENG}
        self.dma_cnt = {}

    def rec(self, eng, fn, reads=(), writes=(), dma=None):
        waits = {}

        def need(dep, raw):
            semkey, val, deng = dep
            if deng == eng and not raw:
                return
            if deng == "dma":
                val = self.dma_cnt[semkey]
            if self.seen[eng].get(semkey, 0) >= val:
                return
            if waits.get(semkey, 0) < val:
                waits[semkey] = val

        for t in reads:
            if t in self.last_w:
                need(self.last_w[t], True)
        for t in writes:
            if t in self.last_w:
                need(self.last_w[t], False)
            for k, (v, de) in self.readers.get(t, {}).items():
                need((k, v, de), False)
        for k, v in waits.items():
            self.seen[eng][k] = v
        if fn is None:
            self.ops[eng].append((list(waits.items()), None, None))
            return
        if dma is None:
            self.seq[eng] += 1
            done = (eng, self.seq[eng], eng)
        else:
            self.dma_cnt[dma] = self.dma_cnt.get(dma, 0) + 16
            done = (dma, self.dma_cnt[dma], "dma")
        self.ops[eng].append((list(waits.items()), fn, done))
        for t in reads:
            self.readers.setdefault(t, {})[done[0]] = (done[1], done[2])
        for t in writes:
            self.last_w[t] = done
            self.readers[t] = {}


def build(ntiles=S // T, stop=3, mstop=9):
    nc = bass.Bass("TRN2", target_bir_lowering=False)
    dt = nc.dram_tensor
    x_d = dt("x", [S, D], F32, kind="ExternalInput").ap()
    f1n_d = dt("ffn1_norm", [D], F32, kind="ExternalInput").ap()
    f1wi_d = dt("ffn1_w_in", [D, 2 * DFF], F32, kind="ExternalInput").ap()
    f1wo_d = dt("ffn1_w_out", [DFF, D], F32, kind="ExternalInput").ap()
    mxn_d = dt("mix_norm", [D], F32, kind="ExternalInput").ap()
    win_d = dt("w_in", [D, 5632], F32, kind="ExternalInput").ap()
    cdk_d = dt("conv_dw_kernel", [CW, D], F32, kind="ExternalInput").ap()
    cdb_d = dt("conv_dw_bias", [D], F32, kind="ExternalInput").ap()
    clg_d = dt("conv_ln_g", [D], F32, kind="ExternalInput").ap()
    clb_d = dt("conv_ln_b", [D], F32, kind="ExternalInput").ap()
    cwp_d = dt("conv_w_proj", [D, D], F32, kind="ExternalInput").ap()
    qn_d = dt("q_norm", [64], F32, kind="ExternalInput").ap()
    kn_d = dt("k_norm", [64], F32, kind="ExternalInput").ap()
    snk_d = dt("attn_sinks", [16], F32, kind="ExternalInput").ap()
    rb_d = dt("rel_bias", [32, 16], F32, kind="ExternalInput").ap()
    awo_d = dt("attn_w_o", [D, D], F32, kind="ExternalInput").ap()
    wout_d = dt("w_out", [D, D], F32, kind="ExternalInput").ap()
    f2n_d = dt("ffn2_norm", [D], F32, kind="ExternalInput").ap()
    f2wi_d = dt("ffn2_w_in", [D, 2 * DFF], F32, kind="ExternalInput").ap()
    f2wo_d = dt("ffn2_w_out", [DFF, D], F32, kind="ExternalInput").ap()
    ident_d = dt("c_ident", [128, 128], F32, kind="ExternalInput").ap()
    oh_d = dt("c_onehot", [32, 128], F32, kind="ExternalInput").ap()
    out_d = dt("out", [S, D], F32, kind="ExternalOutput").ap()
    wscr = dt("wscr", [NSLAB, 128, SLAB_E], BF16).ap()
    a_d = dt("a_scr", [16, 383], F32).ap()
    rbb_h = dt("rb_scr", [16 * 128 * 383], F32)
    rbb = rbb_h.ap()

    kb = KB()
    es = ExitStack()
    sb = lambda name, shape, dtype: es.enter_context(nc.sbuf_tensor(name, shape, dtype))
    with es:
        x_bufs = [sb("x_a", [128, TB, D], F32), sb("x_b", [128, TB, D], F32)]
        h_fm = sb("h_fm", [128, KC, T], BF16)
        hid = sb("hid", [128, FC, T], BF16)
        zb_ = sb("z", [128, KC, 32 + T], BF16)
        qn = sb("qn", [128, KC, T], BF16)
        kn = sb("kn", [128, 4, 128 + T], BF16)
        v_aug = sb("v_aug", [128, TB + 1, 4, 128], BF16)
        cact = sb("cact", [128, KC, T], BF16)
        o_fm = sb("o_fm", [128, KC, T], BF16)
        EB = sb("EB", [128, 4, 2, 512], F32)
        esk = sb("esk", [128, 16], F32)
        wring = sb("wring", [128, R, SLAB_E], BF16)
        NTF, NTB = 6, 6
        tf = sb("tf", [128, NTF, 512], F32)
        tbf = sb("tbf", [128, NTB, 512], BF16)
        ptile = sb("ptile", [128, 4, 512], BF16)
        mgh = sb("mgh", [128, 4, 512], F32)
        lnm = sb("lnm", [128, 512], F32)
        lnr = sb("lnr", [128, 512], F32)
        ident_f = sb("ident_f", [128, 128], F32)
        ident_b = sb("ident_b", [128, 128], BF16)
        ones_b = sb("ones_b", [128, 128], BF16)
        bones_b = sb("bones_b", [128, 128], BF16)
        pvec = sb("pvec", [128, KC, 40], F32)
        gk8 = sb("gk8", [128, 1], F32)
        ss = sb("ss", [128, TB], F32)
        rstd = sb("rstd", [128, TB], F32)
        ssq = sb("ssq", [128, TB], F32)
        oh_sb = sb("oh_sb", [32, 128], F32)
        rb_sb = sb("rb_sb", [32, 16], F32)
        arow = sb("arow", [16, 383], F32)
        ps = es.enter_context(nc.psum_tensor("ps", [128, 8, 512], F32))
        rows = cact[0:40, 0:4, :].rearrange("p a n -> p (a n)").bitcast(F32)
        junk = ptile[:, 0:2, :].rearrange("p a n -> p (a n)")
        xn_v = lambda tb: o_fm[:, 2 * tb:2 * tb + 2, :].rearrange("p a n -> p (a n)")
        XNT = lambda tb: [("o", 2 * tb), ("o", 2 * tb + 1)]
        cur = {"x": x_bufs[0], "p": 0}
        XT = lambda tb, hf: ("x", cur["p"], tb, hf)

        sems = {}
        for k in ["pe", "act", "dve", "pool", "xl", "xs", "misc", "cst"]:
            sems[k] = es.enter_context(nc.semaphore(k))
        for i in range(R):
            sems["w%d" % i] = es.enter_context(nc.semaphore("w%d" % i))
        for i in range(NPREP):
            sems["p%d" % i] = es.enter_context(nc.semaphore("p%d" % i))

        P_BIAS, P_LNG, P_LNB, P_F1, P_MX, P_F2, P_QK = 31, 32, 33, 34, 35, 36, 37

        state = {"bank": 0, "tf": 0, "tb": 0}
        reserved = set()

        def bank():
            for _ in range(16):
                b = state["bank"]
                state["bank"] = (b + 1) % 8
                if b not in reserved:
                    return b
            raise RuntimeError("no bank")

        def tfs():
            s_ = state["tf"]
            state["tf"] = (s_ + 1) % NTF
            return s_

        def tbs():
            s_ = state["tb"]
            state["tb"] = (s_ + 1) % NTB
            return s_

        def psb(b):
            return ps[:, b, 0:256].bitcast(BF16)

        def mm(reads, writes, mms):
            def fn(e):
                inst = None
                for (o, l, r, st, sp) in mms:
                    inst = e.matmul(o, l, r, start=st, stop=sp)
                return inst
            kb.rec("pe", fn, reads, writes)

        def op(eng, reads, writes, f):
            kb.rec(eng, f, reads, writes)

        def dma(eng, semkey, reads, writes, out, in_, **kw):
            kb.rec(eng, lambda e: e.dma_start(out=out, in_=in_, **kw), reads, writes, dma=semkey)

        dma("sp", "xl", [], [("x", 0, tb, hf) for tb in range(TB) for hf in range(2)], x_bufs[0][:],
            x_d[0:T, :].rearrange("(tb p) d -> p tb d", p=128))
        dma("sp", "cst", [], [("ident_f",)], ident_f[:], ident_d)
        dma("sp", "cst", [], [("oh",)], oh_sb[:], oh_d)
        dma("sp", "cst", [], [("rb",)], rb_sb[:], rb_d)
        dma("sp", "cst", [], [("esk",)], esk[:], snk_d.partition_broadcast(128))
        op("pool", [], [("rows",)], lambda e: e.memset(rows, 0.0))
        dma("sp", "cst", [("rows",)], [("rows", 0)], rows[0:CW, :], cdk_d)
        for r_, v_ in [(P_BIAS, cdb_d), (P_LNG, clg_d), (P_LNB, clb_d), (P_F1, f1n_d), (P_MX, mxn_d), (P_F2, f2n_d)]:
            dma("sp", "cst", [("rows",)], [("rows", r_)], rows[r_:r_ + 1, :], v_.rearrange("(o n) -> o n", o=1))
        for i_, v_ in enumerate([qn_d, qn_d, kn_d, kn_d]):
            dma("sp", "cst", [("rows",)], [("rows", 100 + i_)], rows[P_QK:P_QK + 1, 64 * i_:64 * i_ + 64],
                v_.rearrange("(o n) -> o n", o=1))
        ROWTOK = [("rows", 0)] + [("rows", r_) for r_ in range(P_BIAS, P_F2 + 1)] + [("rows", 100 + i_) for i_ in range(4)]
        op("dve", [("ident_f",)], [("ident_b",)], lambda e: e.tensor_copy(out=ident_b[:], in_=ident_f[:]))
        op("pool", [], [("ones_b",)], lambda e: e.memset(ones_b[:], 1.0))
        op("pool", [], [("bones_b",)], lambda e: e.memset(bones_b[:], 0.0))
        op("pool", [("bones_b",)], [("bones_b",)], lambda e: e.memset(bones_b[0:64, 0:64], 1.0))
        op("pool", [("bones_b",)], [("bones_b",)], lambda e: e.memset(bones_b[64:128, 64:128], 1.0))
        for c in range(KC):
            b = bank()
            op("pe", ROWTOK + [("ident_f",)], [("ps", b)],
               lambda e, c=c, b=b: e.transpose(ps[:, b, 0:40], rows[0:40, c * 128:(c + 1) * 128], ident_f[0:40, 0:40]))
            op("dve", [("ps", b)], [("pvec",)], lambda e, c=c, b=b: e.tensor_copy(out=pvec[:, c, :], in_=ps[:, b, 0:40]))
        op("dve", [("pvec",)], [("gk8",)],
           lambda e: e.tensor_scalar(out=gk8[:], in0=pvec[:, 1, P_QK:P_QK + 1], scalar1=8.0, scalar2=None, op0=ALU.mult))
        op("act", [("esk",)], [("esk",)], lambda e: e.activation(out=esk[:], in_=esk[:], func=AF.Exp))
        op("pool", [], [("v", i) for i in range(TB + 1)], lambda e: e.memset(v_aug[:], 1.0))
        op("pool", [], [("zh", c) for c in range(KC)], lambda e: e.memset(zb_[:, :, 0:32], 0.0))
        op("pool", [], [("kn", h, 0) for h in range(4)], lambda e: e.memset(kn[:, :, 0:128], 0.0))
        b = bank()
        op("pe", [("oh",), ("rb",)], [("ps", b)],
           lambda e, b=b: e.matmul(ps[0:16, b, 0:128], rb_sb[:, :], oh_sb[:, :], start=True, stop=True))
        op("pool", [], [("arow",)], lambda e: e.memset(arow[:], 0.0))
        op("act", [("ps", b), ("arow",)], [("arow",)],
           lambda e, b=b: e.activation(out=arow[:, 127:255], in_=ps[0:16, b, 0:128], func=AF.Exp))
        EBTOK = lambda h, xy: [("EB", h, xy, role, gi) for role in range(2) for gi in range(2)]

        prep_n = [0]

        def prep(slab, out, in_, extra_reads=(), eng="pool"):
            semkey = "p%d" % (slab % NPREP)
            dma(eng, semkey, list(extra_reads), [("scr", slab, prep_n[0])], out, in_)
            scr_tok.setdefault(slab, []).append(("scr", slab, prep_n[0]))
            prep_n[0] += 1

        scr_tok = {}
        sv = lambda s_, n: wscr[s_].rearrange("p (k n) -> p k n", n=n)
        wv_ = lambda w: w.rearrange("(k p) n -> p k n", p=128)

        def prep_ffn(up0, dn0, wi, wo, jobs=None):
            wiv = wv_(wi)
            lst = []
            for s_ in range(11):
                lst.append((up0 + s_, sv(up0 + s_, 512)[:, 0:KC, 0:256], wiv[:, :, 256 * s_:256 * s_ + 256]))
                lst.append((up0 + s_, sv(up0 + s_, 512)[:, 0:KC, 256:512], wiv[:, :, DFF + 256 * s_:DFF + 256 * s_ + 256]))
            wov = wv_(wo)
            for half in range(2):
                for ks in range(3):
                    k0, k1 = 8 * ks, min(8 * ks + 8, FC)
                    lst.append((dn0 + half * 3 + ks, sv(dn0 + half * 3 + ks, 512)[:, 0:k1 - k0, :], wov[:, k0:k1, half * 512:(half + 1) * 512]))
            for a_ in lst:
                if jobs is None:
                    prep(*a_)
                else:
                    jobs.append(lambda a_=a_: prep(*a_))

        prep_ffn(F1_UP, F1_DN, f1wi_d, f1wo_d)
        dma("act", "misc", [("arow",)], [("a_d",)], a_d, arow[:])
        dma("act", "misc", [("a_d",)], [("rbb",)],
            bass.AP(rbb_h, 0, [[128 * 383, 16], [383, 128], [1, 383]]),
            bass.AP(a_d.tensor, 0, [[383, 16], [0, 128], [1, 383]]))
        for h in range(4):
            for xy in range(2):
                for role in range(2):
                    c0 = 255 if role == 0 else 127
                    for gi in range(2):
                        j = 4 * h + 2 * gi + xy
                        col = (role * 2 + gi) * 128
                        dma("act", "misc", [("rbb",)], [("EB", h, xy, role, gi)], EB[:, h, xy, col:col + 128],
                            bass.AP(rbb_h, j * 128 * 383 + c0, [[382, 128], [1, 128]]))

        kb.rec("pool", None, reads=[tk for s_ in range(F1_UP, F1_UP + 17) for tk in scr_tok[s_]])
        winv = wv_(win_d)
        for s_ in range(4):
            prep(MX_CV + s_, sv(MX_CV + s_, 512)[:, 0:KC, 0:256], winv[:, :, 256 * s_:256 * s_ + 256])
            prep(MX_CV + s_, sv(MX_CV + s_, 512)[:, 0:KC, 256:512], winv[:, :, 1024 + 256 * s_:1024 + 256 * s_ + 256])
        for s_ in range(2):
            prep(MX_Q + s_, sv(MX_Q + s_, 512)[:, 0:KC, :], winv[:, :, 2048 + 512 * s_:2048 + 512 * s_ + 512])
        for i_ in range(8):
            prep(MX_K, sv(MX_K, 512)[:, 0:KC, 64 * i_:64 * i_ + 64], winv[:, :, 3072 + 64 * (i_ // 2):3072 + 64 * (i_ // 2) + 64])
        prep(MX_V, sv(MX_V, 512)[:, 0:KC, 0:256], winv[:, :, 3328:3584])
        cwpv = wv_(cwp_d)
        for half in range(2):
            base = MX_MG + 4 * half
            prep(base + 0, sv(base + 0, 512)[:, 0:KC, :], winv[:, :, 3584 + 512 * half:3584 + 512 * half + 512])
            prep(base + 1, sv(base + 1, 512)[:, 0:KC, :], winv[:, :, 4608 + 512 * half:4608 + 512 * half + 512])
            prep(base + 2, sv(base + 2, 512)[:, 0:KC, :], cwpv[:, :, 512 * half:512 * half + 512])
            for kc in range(KC):
                ha, hb = head_pair(kc)
                prep(base + 3, sv(base + 3, 512)[0:64, kc, :], awo_d[64 * ha:64 * ha + 64, 512 * half:512 * half + 512])
                prep(base + 3, sv(base + 3, 512)[64:128, kc, :], awo_d[64 * hb:64 * hb + 64, 512 * half:512 * half + 512])
        woutv = wv_(wout_d)
        for half in range(2):
            prep(MX_WO + half, sv(MX_WO + half, 512)[:, 0:KC, :], woutv[:, :, 512 * half:512 * half + 512])
        def diag_jobs():
            jobs = []
            for c in range(KC):
                for g4 in range(4):
                    def job(c=c, g4=g4):
                        ts = tfs()
                        stage = tf[:, ts, :].bitcast(BF16).rearrange("p (j n) -> p j n", n=128)
                        taps = range(8 * g4, min(8 * g4 + 8, CW))

                        def fnd(e, stage=stage, taps=taps, c=c):
                            inst = None
                            for jj, j in enumerate(taps):
                                inst = e.tensor_scalar(out=stage[:, jj, :], in0=ident_b[:], scalar1=pvec[:, c, j:j + 1],
                                                       scalar2=None, op0=ALU.mult)
                            return inst
                        op("dve", [("pvec",), ("ident_b",)], [("tf", ts)], fnd)
                        n_ = len(taps)
                        prep(MX_DG + c, sv(MX_DG + c, 128)[:, 8 * g4:8 * g4 + n_, :], stage[:, 0:n_, :],
                             extra_reads=[("tf", ts)], eng="act")
                    jobs.append(job)
            return jobs

        tile_sched = (list(range(F1_UP, F1_UP + 17)) + list(range(MX_CV, MX_CV + 4)) + [MX_Q, MX_Q + 1, MX_K, MX_V]
                      + list(range(MX_DG, MX_DG + 8)) + list(range(MX_MG, MX_MG + 8)) + [MX_WO, MX_WO + 1]
                      + list(range(F2_UP, F2_UP + 17)))
        if stop == 1:
            tile_sched = tile_sched[:17]
        elif stop == 2:
            tile_sched = tile_sched[:17 + 26]
        elif stop == 0:
            tile_sched = []
        sched = tile_sched * ntiles
        slab_ne = {}
        for s_ in range(NSLAB):
            slab_ne[s_] = SLAB_E
        for base in (F1_DN, F2_DN):
            for half in range(2):
                slab_ne[base + half * 3 + 2] = 6 * 512
        slab_ne[MX_V] = SLAB_E
        ring = {"pos": 0, "loaded": 0, "done": -1}
        PFMAX = 2

        def wget(slab, keep=False):
            pos = ring["pos"]
            assert sched[pos] == slab, (pos, sched[pos], slab)
            ring["pos"] += 1
            if not keep:
                ring["done"] = pos - 1
            while (ring["loaded"] <= min(pos + PFMAX, len(sched) - 1)
                   and (ring["loaded"] - R <= ring["done"] or ring["loaded"] <= pos)):
                i = ring["loaded"]
                assert i - R <= ring["done"], "weight ring too small"
                sl = i % R
                s2 = sched[i]
                ne = slab_ne[s2]
                pk = "p%d" % (s2 % NPREP)
                kb.last_w[("scrall", s2)] = (pk, kb.dma_cnt[pk], "dma")
                dma("sp", "w%d" % sl, [("scrall", s2)], [("w", sl)], wring[:, sl, 0:ne], wscr[s2, :, 0:ne])
                ring["loaded"] += 1
            return pos % R

        HTOK = [("h", c) for c in range(KC)]

        def rmsnorm(prow):
            x_tm = cur["x"]
            for tb in range(TB):
                op("act", [XT(tb, 0), XT(tb, 1)], [("pt", 0), ("pt", 1), ("ss", tb)],
                   lambda e, tb=tb: e.activation(out=junk, in_=x_tm[:, tb, :], func=AF.Square, accum_out=ss[:, tb:tb + 1]))
                op("act", [("ss", tb)], [("ssq", tb)],
                   lambda e, tb=tb: e.activation(out=ssq[:, tb:tb + 1], in_=ss[:, tb:tb + 1], func=AF.Ln, bias=float(D * EPS)))
                op("act", [("ssq", tb)], [("rstd", tb)],
                   lambda e, tb=tb: e.activation(out=rstd[:, tb:tb + 1], in_=ssq[:, tb:tb + 1], func=AF.Exp, scale=-0.5))
                op("dve", [XT(tb, 0), XT(tb, 1), ("rstd", tb)], XNT(tb),
                   lambda e, tb=tb: e.tensor_scalar(out=xn_v(tb), in0=x_tm[:, tb, :], scalar1=rstd[:, tb:tb + 1],
                                                    scalar2=None, op0=ALU.mult))
            for c in range(KC):
                b = bank()

                def fn(e, c=c, b=b):
                    inst = None
                    for tb in range(TB):
                        inst = e.transpose(psb(b)[:, tb * 128:(tb + 1) * 128], xn_v(tb)[:, c * 128:(c + 1) * 128], ident_b[:])
                    return inst
                op("pe", [tk for tb in range(TB) for tk in XNT(tb)] + [("ident_b",)], [("ps", b)], fn)
                op("dve", [("ps", b), ("pvec",)], [("h", c)],
                   lambda e, c=c, b=b: e.tensor_scalar(out=h_fm[:, c, :], in0=psb(b), scalar1=pvec[:, c, prow:prow + 1],
                                                       scalar2=float(math.sqrt(D)), op0=ALU.mult, op1=ALU.mult))

        def ffn(up0, dn0, extra=None):
            for s_ in range(11):
                sl = wget(up0 + s_)
                wv = wring[:, sl, :].rearrange("p (k n) -> p k n", n=512)
                for jj in range(2):
                    i = 2 * s_ + jj
                    ba, bb = bank(), bank()
                    mm([("w", sl)] + HTOK, [("ps", ba)],
                       [(ps[:, ba, :], wv[:, kc, jj * 128:(jj + 1) * 128], h_fm[:, kc, :], kc == 0, kc == KC - 1) for kc in range(KC)])
                    mm([("w", sl)] + HTOK, [("ps", bb)],
                       [(ps[:, bb, :], wv[:, kc, 256 + jj * 128:256 + (jj + 1) * 128], h_fm[:, kc, :], kc == 0, kc == KC - 1) for kc in range(KC)])
                    ts = tfs()
                    op("act", [("ps", ba)], [("tf", ts)],
                       lambda e, ba=ba, ts=ts: e.activation(out=tf[:, ts, :], in_=ps[:, ba, :], func=AF.Silu))
                    op("dve", [("ps", bb), ("tf", ts)], [("hid", i)],
                       lambda e, bb=bb, ts=ts, i=i: e.tensor_tensor(out=hid[:, i, :], in0=ps[:, bb, :], in1=tf[:, ts, :], op=ALU.mult))
                    if extra:
                        extra.pop(0)()
            while extra:
                extra.pop(0)()
            for half in range(2):
                banks = []
                for tb in range(TB):
                    b = bank()
                    reserved.add(b)
                    banks.append(b)
                for ks in range(3):
                    sl = wget(dn0 + half * 3 + ks)
                    wv = wring[:, sl, :].rearrange("p (k n) -> p k n", n=512)
                    k0, k1 = 8 * ks, min(8 * ks + 8, FC)
                    for tb in range(TB):
                        mm([("w", sl)] + [("hid", kc) for kc in range(k0, k1)], [("ps", banks[tb])],
                           [(ps[:, banks[tb], :], hid[:, kc, tb * 128:(tb + 1) * 128], wv[:, kc - k0, :], kc == 0, kc == FC - 1)
                            for kc in range(k0, k1)])
                for tb in range(TB):
                    b = banks[tb]
                    xs_ = cur["x"][:, tb, half * 512:(half + 1) * 512]
                    op("dve", [("ps", b), XT(tb, half)], [XT(tb, half)],
                       lambda e, b=b, xs_=xs_: e.scalar_tensor_tensor(out=xs_, in0=ps[:, b, :], scalar=0.5, in1=xs_,
                                                                      op0=ALU.mult, op1=ALU.add))
                    reserved.discard(b)

        def qk_stage_a(wv, col0):
            bq = bank()
            mm([("w", wv[1])] + HTOK, [("ps", bq)],
               [(ps[:, bq, :], wv[0][:, kc, col0:col0 + 128], h_fm[:, kc, :], kc == 0, kc == KC - 1) for kc in range(KC)])
            t2 = tbs()
            op("act", [("ps", bq)], [("tb", t2)], lambda e: e.activation(out=tbf[:, t2, :], in_=ps[:, bq, :], func=AF.Square))
            return bq, t2

        def qk_stage_b(st, gain_ap, out_ap, out_tok):
            bq, t2 = st
            bs = bank()
            mm([("tb", t2), ("bones_b",)], [("ps", bs)], [(ps[:, bs, :], bones_b[:], tbf[:, t2, :], True, True)])
            t3, t3b = tfs(), tfs()
            op("act", [("ps", bs)], [("tf", t3b)],
               lambda e: e.activation(out=tf[:, t3b, :], in_=ps[:, bs, :], func=AF.Ln, bias=float(64 * EPS)))
            op("act", [("tf", t3b)], [("tf", t3)],
               lambda e: e.activation(out=tf[:, t3, :], in_=tf[:, t3b, :], func=AF.Exp, scale=-0.5))
            op("dve", [("ps", bq), ("tf", t3), ("pvec",), ("gk8",)], out_tok,
               lambda e: e.scalar_tensor_tensor(out=out_ap, in0=ps[:, bq, :], scalar=gain_ap, in1=tf[:, t3, :],
                                                op0=ALU.mult, op1=ALU.mult))

        def mixer(t):
            for s_ in range(4):
                sl = wget(MX_CV + s_)
                wv = wring[:, sl, :].rearrange("p (k n) -> p k n", n=512)
                for jj in range(2):
                    c = 2 * s_ + jj
                    bg, ba = bank(), bank()
                    mm([("w", sl)] + HTOK, [("ps", bg)],
                       [(ps[:, bg, :], wv[:, kc, 256 + jj * 128:256 + (jj + 1) * 128], h_fm[:, kc, :], kc == 0, kc == KC - 1) for kc in range(KC)])
                    mm([("w", sl)] + HTOK, [("ps", ba)],
                       [(ps[:, ba, :], wv[:, kc, jj * 128:(jj + 1) * 128], h_fm[:, kc, :], kc == 0, kc == KC - 1) for kc in range(KC)])
                    ts = tfs()
                    op("act", [("ps", bg)], [("tf", ts)],
                       lambda e, bg=bg, ts=ts: e.activation(out=tf[:, ts, :], in_=ps[:, bg, :], func=AF.Sigmoid))
                    op("dve", [("ps", ba), ("tf", ts)], [("z", c)],
                       lambda e, ba=ba, ts=ts, c=c: e.tensor_tensor(out=zb_[:, c, 32:32 + T], in0=ps[:, ba, :], in1=tf[:, ts, :], op=ALU.mult))
            if mstop <= 1:
                return
            jobs = []
            for s_ in range(2):
                for jj in range(4):
                    c = 4 * s_ + jj
                    jobs.append((MX_Q + s_, jj * 128, pvec[:, 0, P_QK:P_QK + 1], qn[:, c, :], [("qn", c)]))
            for h in range(4):
                jobs.append((MX_K, h * 128, gk8[:, 0:1], kn[:, h, 128:128 + T], [("kn", h, 1 + tb) for tb in range(TB)]))
            cur_slab, sl, wv = None, None, None
            pend = []
            for (slab, col0, gain_ap, out_ap, out_tok) in jobs:
                if slab != cur_slab:
                    sl = wget(slab)
                    wv = wring[:, sl, :].rearrange("p (k n) -> p k n", n=512)
                    cur_slab = slab
                st = qk_stage_a((wv, sl), col0)
                pend.append((st, gain_ap, out_ap, out_tok))
                if len(pend) > 1:
                    qk_stage_b(*pend.pop(0))
            while pend:
                qk_stage_b(*pend.pop(0))
            if mstop <= 3:
                return
            sl = wget(MX_V)
            wv = wring[:, sl, :].rearrange("p (k n) -> p k n", n=512)
            for tb in range(TB):
                b = bank()
                mm([("w", sl)] + HTOK, [("ps", b)],
                   [(ps[:, b, 0:256], h_fm[:, kc, tb * 128:(tb + 1) * 128], wv[:, kc, 0:256], kc == 0, kc == KC - 1) for kc in range(KC)])
                def fnv(e, b=b, tb=tb):
                    inst = None
                    for h in range(4):
                        off = 0 if h % 2 == 0 else 64
                        inst = e.activation(out=v_aug[:, 1 + tb, h, off:off + 64], in_=ps[:, b, 64 * h:64 * h + 64], func=AF.Identity)
                    return inst
                op("act", [("ps", b)], [("v", 1 + tb)], fnv)
            b1 = bank()
            reserved.add(b1)
            b2 = bank()
            reserved.add(b2)

            def conv_a(c):
                sl = wget(MX_DG + c)
                dv = wring[:, sl, :].rearrange("p (j n) -> p j n", n=128)
                b = bank()
                mm([("w", sl), ("z", c), ("zh", c)], [("ps", b)],
                   [(ps[:, b, :], dv[:, j, :], zb_[:, c, 2 + j:2 + j + T], j == 0, j == CW - 1) for j in range(CW)])
                zc = hid[:, 2 * c:2 * c + 2, :].rearrange("p a n -> p (a n)").bitcast(F32)
                t1, t2 = tbs(), tbs()
                bias_ap = pvec[:, c, P_BIAS:P_BIAS + 1]
                op("act", [("ps", b), ("pvec",)], [("hid", 2 * c), ("hid", 2 * c + 1)],
                   lambda e: e.activation(out=zc, in_=ps[:, b, :], func=AF.Identity, bias=bias_ap))
                op("act", [("ps", b), ("pvec",)], [("tb", t1)],
                   lambda e: e.activation(out=tbf[:, t1, :], in_=ps[:, b, :], func=AF.Identity, bias=bias_ap))
                op("act", [("ps", b), ("pvec",)], [("tb", t2)],
                   lambda e: e.activation(out=tbf[:, t2, :], in_=ps[:, b, :], func=AF.Square, bias=bias_ap))
                return t1, t2

            def conv_b(c, t1, t2):
                mm([("tb", t1), ("ones_b",)], [("ps", b1)], [(ps[:, b1, :], ones_b[:], tbf[:, t1, :], c == 0, c == KC - 1)])
                mm([("tb", t2), ("ones_b",)], [("ps", b2)], [(ps[:, b2, :], ones_b[:], tbf[:, t2, :], c == 0, c == KC - 1)])

            def att_a(n, h, slot0):
                gblk = t * TB + n
                roles = [1] if gblk == 0 else [0, 1]
                lo = 256 if gblk == 0 else 0
                bxy = [bank(), bank()]
                mms = []
                for role in roles:
                    kcol = (n + role) * 128
                    for gi in range(2):
                        for xy in range(2):
                            j = 4 * h + 2 * gi + xy
                            cq, pb = j // 2, 64 * xy
                            col = (role * 2 + gi) * 128
                            mms.append((ps[:, bxy[xy], col:col + 128], kn[pb:pb + 64, h, kcol:kcol + 128],
                                        qn[pb:pb + 64, cq, n * 128:(n + 1) * 128], True, True))
                mm([("kn", h, n + r_) for r_ in roles] + [("qn", 2 * h), ("qn", 2 * h + 1)], [("ps", bxy[0]), ("ps", bxy[1])], mms)
                pts = []
                for xy in range(2):
                    t1 = tbs()
                    t2 = slot0 + xy
                    op("act", [("ps", bxy[xy])], [("tb", t1)],
                       lambda e, b=bxy[xy], t1=t1: e.activation(out=tbf[:, t1, lo:512], in_=ps[:, b, lo:512], func=AF.Exp))
                    op("pool", [("tb", t1)] + EBTOK(h, xy), [("pt", t2)],
                       lambda e, t1=t1, t2=t2, xy=xy: e.tensor_tensor(out=ptile[:, t2, lo:512], in0=tbf[:, t1, lo:512],
                                                                      in1=EB[:, h, xy, lo:512], op=ALU.mult))
                    pts.append(t2)
                return roles, pts

            def att_b(n, h, roles, pts):
                bo = bank()
                mms = []
                for xy in range(2):
                    for r_ in roles:
                        mms.append((ps[:, bo, xy * 256:(xy + 1) * 256], v_aug[:, n + r_, h, :],
                                    ptile[:, pts[xy], r_ * 256:(r_ + 1) * 256], r_ == roles[0], r_ == roles[-1]))
                mm([("pt", pts[0]), ("pt", pts[1])] + [("v", n + r_) for r_ in roles], [("ps", bo)], mms)
                ob = 0 if h % 2 == 0 else 64
                db = 64 - ob
                t3, t4 = tfs(), tfs()

                def fna(e):
                    inst = None
                    for i in range(4):
                        j = 4 * h + 2 * (i % 2) + (i // 2)
                        inst = e.activation(out=tf[ob:ob + 64, t3, i * 128:(i + 1) * 128],
                                            in_=ps[db:db + 64, bo, i * 128:(i + 1) * 128],
                                            func=AF.Ln, bias=esk[db:db + 64, j:j + 1])
                    return inst
                op("act", [("ps", bo), ("esk",)], [("tf", t3)], fna)
                op("act", [("tf", t3)], [("tf", t4)],
                   lambda e: e.activation(out=tf[ob:ob + 64, t4, :], in_=tf[ob:ob + 64, t3, :], func=AF.Exp, scale=-1.0))
                c0 = (0 if h < 2 else 4)
                op("dve", [("ps", bo), ("tf", t4)], [("o", c0 + g) for g in range(4)],
                   lambda e: e.tensor_tensor(
                       out=o_fm[ob:ob + 64, c0:c0 + 4, n * 128:(n + 1) * 128],
                       in0=ps[ob:ob + 64, bo, :].rearrange("p (g q) -> p g q", q=128),
                       in1=tf[ob:ob + 64, t4, :].rearrange("p (g q) -> p g q", q=128), op=ALU.mult))

            items = [(n, h) for n in range(TB) for h in range(4)]
            f2jobs = []
            if t == 0 and stop >= 3:
                prep_ffn(F2_UP, F2_DN, f2wi_d, f2wo_d, jobs=f2jobs)
            for c in range(KC):
                for _ in range(2):
                    if f2jobs:
                        f2jobs.pop(0)()
                ia, ib = items[2 * c], items[2 * c + 1]
                sa = att_a(ia[0], ia[1], 0)
                sb_ = att_a(ib[0], ib[1], 2)
                ct = conv_a(c)
                att_b(ia[0], ia[1], *sa)
                att_b(ib[0], ib[1], *sb_)
                conv_b(c, *ct)
            t3, t4 = tfs(), tfs()
            op("dve", [("ps", b1)], [("lnm",)],
               lambda e: e.tensor_scalar(out=lnm[:], in0=ps[:, b1, :], scalar1=1.0 / D, scalar2=None, op0=ALU.mult))
            op("dve", [("lnm",)], [("tf", t3)], lambda e: e.tensor_tensor(out=tf[:, t3, :], in0=lnm[:], in1=lnm[:], op=ALU.mult))
            op("dve", [("ps", b2), ("tf", t3)], [("tf", t4)],
               lambda e: e.scalar_tensor_tensor(out=tf[:, t4, :], in0=ps[:, b2, :], scalar=1.0 / D, in1=tf[:, t3, :],
                                                op0=ALU.mult, op1=ALU.subtract))
            t4b = tfs()
            op("act", [("tf", t4)], [("tf", t4b)],
               lambda e: e.activation(out=tf[:, t4b, :], in_=tf[:, t4, :], func=AF.Ln, bias=float(EPS)))
            op("act", [("tf", t4b)], [("lnr",)], lambda e: e.activation(out=lnr[:], in_=tf[:, t4b, :], func=AF.Exp, scale=-0.5))
            reserved.discard(b1)
            reserved.discard(b2)
            for c in range(KC):
                zc = hid[:, 2 * c:2 * c + 2, :].rearrange("p a n -> p (a n)").bitcast(F32)
                t5, t6 = tfs(), tfs()
                if f2jobs:
                    f2jobs.pop(0)()
                op("pool", [("hid", 2 * c), ("hid", 2 * c + 1), ("lnm",)], [("tf", t5)],
                   lambda e, zc=zc, t5=t5: e.tensor_tensor(out=tf[:, t5, :], in0=zc, in1=lnm[:], op=ALU.subtract))
                op("dve", [("tf", t5), ("lnr",)], [("tf", t6)],
                   lambda e, t5=t5, t6=t6: e.tensor_tensor(out=tf[:, t6, :], in0=tf[:, t5, :], in1=lnr[:], op=ALU.mult))
                op("act", [("tf", t6), ("pvec",)], [("cact", c)],
                   lambda e, c=c, t6=t6: e.activation(out=cact[:, c, :], in_=tf[:, t6, :], func=AF.Silu,
                                                      scale=pvec[:, c, P_LNG:P_LNG + 1], bias=pvec[:, c, P_LNB:P_LNB + 1]))
            if mstop <= 6:
                return
            for half in range(2):
                base = MX_MG + 4 * half
                sls = [wget(base + i, keep=(i > 0)) for i in range(4)]
                wvs = [wring[:, sl, :].rearrange("p (k n) -> p k n", n=512) for sl in sls]
                OTOK = [("o", c) for c in range(KC)]
                CTOK = [("cact", c) for c in range(KC)]

                def grp(i, f4, src, stok):
                    b = bank()
                    mm([("w", sls[i])] + stok, [("ps", b)],
                       [(ps[:, b, :], wvs[i][:, kc, f4 * 128:(f4 + 1) * 128], src[:, kc, :], kc == 0, kc == KC - 1) for kc in range(KC)])
                    return b

                def mg_x(f4):
                    p = f4 % 2
                    bgc = grp(0, f4, h_fm, HTOK)
                    bga = grp(1, f4, h_fm, HTOK)
                    bb_ = grp(3, f4, o_fm, OTOK)
                    t2 = tfs()
                    op("act", [("ps", bgc)], [("mgh", 2 * p)],
                       lambda e: e.activation(out=mgh[:, 2 * p, :], in_=ps[:, bgc, :], func=AF.Sigmoid))
                    op("act", [("ps", bga)], [("tf", t2)],
                       lambda e: e.activation(out=tf[:, t2, :], in_=ps[:, bga, :], func=AF.Sigmoid))
                    op("dve", [("ps", bb_), ("tf", t2)], [("mgh", 2 * p + 1)],
                       lambda e: e.tensor_tensor(out=mgh[:, 2 * p + 1, :], in0=ps[:, bb_, :], in1=tf[:, t2, :], op=ALU.mult))

                def mg_y(f4):
                    p = f4 % 2
                    fc = 4 * half + f4
                    ba_ = grp(2, f4, cact, CTOK)
                    t3 = tfs()
                    op("dve", [("ps", ba_), ("mgh", 2 * p)], [("tf", t3)],
                       lambda e: e.tensor_tensor(out=tf[:, t3, :], in0=ps[:, ba_, :], in1=mgh[:, 2 * p, :], op=ALU.mult))
                    op("pool", [("tf", t3), ("mgh", 2 * p + 1)], [("qn", fc)],
                       lambda e: e.tensor_tensor(out=qn[:, fc, :], in0=tf[:, t3, :], in1=mgh[:, 2 * p + 1, :], op=ALU.add))

                while half == 0 and f2jobs:
                    f2jobs.pop(0)()
                mg_x(0)
                mg_x(1)
                mg_y(0)
                mg_x(2)
                mg_y(1)
                mg_x(3)
                mg_y(2)
                mg_y(3)
            for half in range(2):
                sl = wget(MX_WO + half)
                wv = wring[:, sl, :].rearrange("p (k n) -> p k n", n=512)
                for tb in range(TB):
                    b = bank()
                    mm([("w", sl)] + [("qn", c) for c in range(KC)], [("ps", b)],
                       [(ps[:, b, :], qn[:, kc, tb * 128:(tb + 1) * 128], wv[:, kc, :], kc == 0, kc == KC - 1) for kc in range(KC)])
                    xs_ = cur["x"][:, tb, half * 512:(half + 1) * 512]
                    op("dve", [("ps", b), XT(tb, half)], [XT(tb, half)],
                       lambda e, b=b, xs_=xs_: e.tensor_tensor(out=xs_, in0=ps[:, b, :], in1=xs_, op=ALU.add))
            op("pool", [("kn", h, TB) for h in range(4)], [("kn", h, 0) for h in range(4)],
               lambda e: e.tensor_copy(out=kn[:, :, 0:128], in_=kn[:, :, T:T + 128]))
            op("pool", [("v", TB)], [("v", 0)], lambda e: e.tensor_copy(out=v_aug[:, 0, :, :], in_=v_aug[:, TB, :, :]))
            op("pool", [("z", c) for c in range(KC)], [("zh", c) for c in range(KC)],
               lambda e: e.tensor_copy(out=zb_[:, :, 0:32], in_=zb_[:, :, T:T + 32]))

        for t in range(ntiles):
            if stop >= 1:
                rmsnorm(P_F1)
                ffn(F1_UP, F1_DN, extra=(diag_jobs() if (t == 0 and stop >= 2) else None))
            if t + 1 < ntiles:
                nb = 1 - cur["p"]
                dma("sp", "xl", [], [("x", nb, tb, hf) for tb in range(TB) for hf in range(2)], x_bufs[nb][:],
                    x_d[(t + 1) * T:(t + 2) * T, :].rearrange("(tb p) d -> p tb d", p=128))
            if stop >= 2:
                rmsnorm(P_MX)
                mixer(t)
            if stop >= 3:
                rmsnorm(P_F2)
                ffn(F2_UP, F2_DN)
            for tb in range(TB):
                dma("sp", "xs", [XT(tb, 0), XT(tb, 1)], [("out", t, tb)],
                    out_d[t * T + tb * 128:t * T + (tb + 1) * 128, :], cur["x"][:, tb, :])
            cur["p"] = 1 - cur["p"]
            cur["x"] = x_bufs[cur["p"]]
        kb.rec("sp", None, reads=[("out", t, tb) for t in range(ntiles) for tb in range(TB)])

        with nc.Block() as block:
            def emit(eng_name):
                def body(e):
                    for waits, fn, done in kb.ops[eng_name]:
                        for k, v in waits:
                            e.wait_ge(sems[k], v)
                        if fn is None:
                            continue
                        inst = fn(e)
                        inst.then_inc(sems[done[0]], 16 if done[2] == "dma" else 1)
                return body
            block.tensor(emit("pe"))
            block.scalar(emit("act"))
            block.vector(emit("dve"))
            block.gpsimd(emit("pool"))
            block.sync(emit("sp"))
    return nc


def t5_bucket_onehot():
    oh = np.zeros((32, 128), np.float32)
    for d in range(128):
        if d < 16:
            b = d
        else:
            dd = np.float32(max(d, 1))
            val = np.log(dd / np.float32(16)) / np.float32(math.log(128 / 16)) * np.float32(16)
            b = min(16 + int(np.float32(val)), 31)
        oh[b, d] = 1.0
    return oh


_NC_CACHE = {}


def kernel(**inputs):
    ncores = 8
    if "nc" not in _NC_CACHE:
        _NC_CACHE["nc"] = build()
    nc = _NC_CACHE["nc"]
    x = np.ascontiguousarray(np.asarray(inputs["x"], dtype=np.float32))
    consts = {"c_ident": np.eye(128, dtype=np.float32), "c_onehot": t5_bucket_onehot()}
    shared = {k: np.ascontiguousarray(np.asarray(v, dtype=np.float32)) for k, v in inputs.items() if k != "x"}
    in_maps = []
    for b in range(ncores):
        m = dict(shared)
        m.update(consts)
        m["x"] = x[b]
        in_maps.append(m)
    res = run_bass_kernel_spmd(nc, in_maps, core_ids=list(range(ncores)))
    return np.stack([np.asarray(r["out"], dtype=np.float32) for r in res.results], axis=0)
```

```python
import math
from contextlib import ExitStack

import numpy as np
import concourse.bass as bass
import concourse.mybir as mybir
from concourse.bass_utils import run_bass_kernel_spmd

F32 = mybir.dt.float32
BF16 = mybir.dt.bfloat16
AF = mybir.ActivationFunctionType
ALU = mybir.AluOpType

S = 4096
D = 1024
DFF = 2816
T = 512
TB = T // 128
KC = D // 128
FC = DFF // 128
EPS = 1e-6
CW = 31
R = 6
PF = 2
NSLAB = 60
NPREP = 60
SLAB_E = 4096

F1_UP, F1_DN = 0, 11
MX_CV, MX_Q, MX_K, MX_V = 17, 21, 23, 24
MX_MG, MX_WO, MX_DG = 25, 33, 35
F2_UP, F2_DN = 43, 54


GPERM = [0, 2, 1, 3]


def head_pair(kc):
    if kc < 4:
        return GPERM[kc], 4 + GPERM[kc]
    return 8 + GPERM[kc - 4], 12 + GPERM[kc - 4]


class KB:
    ENG = ["pe", "act", "dve", "pool", "sp"]

    def __init__(self):
        self.ops = {e: [] for e in self.ENG}
        self.seq = {e: 0 for e in self.ENG}
        self.last_w = {}
        self.readers = {}
        self.seen = {e: {} for e in self.ENG}
        self.dma_cnt = {}

    def rec(self, eng, fn, reads=(), writes=(), dma=None):
        waits = {}

        def need(dep, raw):
            semkey, val, deng = dep
            if deng == eng and not raw:
                return
            if deng == "dma":
                val = self.dma_cnt[semkey]
            if self.seen[eng].get(semkey, 0) >= val:
                return
            if waits.get(semkey, 0) < val:
                waits[semkey] = val

        for t in reads:
            if t in self.last_w:
                need(self.last_w[t], True)
        for t in writes:
            if t in self.last_w:
                need(self.last_w[t], False)
            for k, (v, de) in self.readers.get(t, {}).items():
                need((k, v, de), False)
        for k, v in waits.items():
            self.seen[eng][k] = v
        if fn is None:
            self.ops[eng].append((list(waits.items()), None, None))
            return
        if dma is None:
            self.seq[eng] += 1
            done = (eng, self.seq[eng], eng)
        else:
            self.dma_cnt[dma] = self.dma_cnt.get(dma, 0) + 16
            done = (dma, self.dma_cnt[dma], "dma")
        self.ops[eng].append((list(waits.items()), fn, done))
        for t in reads:
            self.readers.setdefault(t, {})[done[0]] = (done[1], done[2])
        for t in writes:
            self.last_w[t] = done
            self.readers[t] = {}


def build(ntiles=S // T, stop=3, mstop=9):
    nc = bass.Bass("TRN2", target_bir_lowering=False)
    dt = nc.dram_tensor
    x_d = dt("x", [S, D], F32, kind="ExternalInput").ap()
    f1n_d = dt("ffn1_norm", [D], F32, kind="ExternalInput").ap()
    f1wi_d = dt("ffn1_w_in", [D, 2 * DFF], F32, kind="ExternalInput").ap()
    f1wo_d = dt("ffn1_w_out", [DFF, D], F32, kind="ExternalInput").ap()
    mxn_d = dt("mix_norm", [D], F32, kind="ExternalInput").ap()
    win_d = dt("w_in", [D, 5632], F32, kind="ExternalInput").ap()
    cdk_d = dt("conv_dw_kernel", [CW, D], F32, kind="ExternalInput").ap()
    cdb_d = dt("conv_dw_bias", [D], F32, kind="ExternalInput").ap()
    clg_d = dt("conv_ln_g", [D], F32, kind="ExternalInput").ap()
    clb_d = dt("conv_ln_b", [D], F32, kind="ExternalInput").ap()
    cwp_d = dt("conv_w_proj", [D, D], F32, kind="ExternalInput").ap()
    qn_d = dt("q_norm", [64], F32, kind="ExternalInput").ap()
    kn_d = dt("k_norm", [64], F32, kind="ExternalInput").ap()
    snk_d = dt("attn_sinks", [16], F32, kind="ExternalInput").ap()
    rb_d = dt("rel_bias", [32, 16], F32, kind="ExternalInput").ap()
    awo_d = dt("attn_w_o", [D, D], F32, kind="ExternalInput").ap()
    wout_d = dt("w_out", [D, D], F32, kind="ExternalInput").ap()
    f2n_d = dt("ffn2_norm", [D], F32, kind="ExternalInput").ap()
    f2wi_d = dt("ffn2_w_in", [D, 2 * DFF], F32, kind="ExternalInput").ap()
    f2wo_d = dt("ffn2_w_out", [DFF, D], F32, kind="ExternalInput").ap()
    ident_d = dt("c_ident", [128, 128], F32, kind="ExternalInput").ap()
    oh_d = dt("c_onehot", [32, 128], F32, kind="ExternalInput").ap()
    out_d = dt("out", [S, D], F32, kind="ExternalOutput").ap()
    wscr = dt("wscr", [NSLAB, 128, SLAB_E], BF16).ap()
    a_d = dt("a_scr", [16, 383], F32).ap()
    rbb_h = dt("rb_scr", [16 * 128 * 383], F32)
    rbb = rbb_h.ap()

    kb = KB()
    es = ExitStack()
    sb = lambda name, shape, dtype: es.enter_context(nc.sbuf_tensor(name, shape, dtype))
    with es:
        x_bufs = [sb("x_a", [128, TB, D], F32), sb("x_b", [128, TB, D], F32)]
        h_fm = sb("h_fm", [128, KC, T], BF16)
        hid = sb("hid", [128, FC, T], BF16)
        zb_ = sb("z", [128, KC, 32 + T], BF16)
        qn = sb("qn", [128, KC, T], BF16)
        kn = sb("kn", [128, 4, 128 + T], BF16)
        v_aug = sb("v_aug", [128, TB + 1, 4, 128], BF16)
        cact = sb("cact", [128, KC, T], BF16)
        o_fm = sb("o_fm", [128, KC, T], BF16)
        EB = sb("EB", [128, 4, 2, 512], F32)
        esk = sb("esk", [128, 16], F32)
        wring = sb("wring", [128, R, SLAB_E], BF16)
        NTF, NTB = 6, 6
        tf = sb("tf", [128, NTF, 512], F32)
        tbf = sb("tbf", [128, NTB, 512], BF16)
        ptile = sb("ptile", [128, 4, 512], BF16)
        mgh = sb("mgh", [128, 4, 512], F32)
        lnm = sb("lnm", [128, 512], F32)
        lnr = sb("lnr", [128, 512], F32)
        ident_f = sb("ident_f", [128, 128], F32)
        ident_b = sb("ident_b", [128, 128], BF16)
        ones_b = sb("ones_b", [128, 128], BF16)
        bones_b = sb("bones_b", [128, 128], BF16)
        pvec = sb("pvec", [128, KC, 40], F32)
        gk8 = sb("gk8", [128, 1], F32)
        ss = sb("ss", [128, TB], F32)
        rstd = sb("rstd", [128, TB], F32)
        ssq = sb("ssq", [128, TB], F32)
        oh_sb = sb("oh_sb", [32, 128], F32)
        rb_sb = sb("rb_sb", [32, 16], F32)
        arow = sb("arow", [16, 383], F32)
        ps = es.enter_context(nc.psum_tensor("ps", [128, 8, 512], F32))
        rows = cact[0:40, 0:4, :].rearrange("p a n -> p (a n)").bitcast(F32)
        junk = ptile[:, 0:2, :].rearrange("p a n -> p (a n)")
        xn_v = lambda tb: o_fm[:, 2 * tb:2 * tb + 2, :].rearrange("p a n -> p (a n)")
        XNT = lambda tb: [("o", 2 * tb), ("o", 2 * tb + 1)]
        cur = {"x": x_bufs[0], "p": 0}
        XT = lambda tb, hf: ("x", cur["p"], tb, hf)

        sems = {}
        for k in ["pe", "act", "dve", "pool", "xl", "xs", "misc", "cst", "eb"]:
            sems[k] = es.enter_context(nc.semaphore(k))
        for i in range(R):
            sems["w%d" % i] = es.enter_context(nc.semaphore("w%d" % i))
        for i in range(NPREP):
            sems["p%d" % i] = es.enter_context(nc.semaphore("p%d" % i))

        P_BIAS, P_LNG, P_LNB, P_F1, P_MX, P_F2, P_QK = 31, 32, 33, 34, 35, 36, 37

        state = {"bank": 0, "tf": 0, "tb": 0}
        reserved = set()

        def bank():
            for _ in range(16):
                b = state["bank"]
                state["bank"] = (b + 1) % 8
                if b not in reserved:
                    return b
            raise RuntimeError("no bank")

        def tfs():
            s_ = state["tf"]
            state["tf"] = (s_ + 1) % NTF
            return s_

        def tbs():
            s_ = state["tb"]
            state["tb"] = (s_ + 1) % NTB
            return s_

        def psb(b):
            return ps[:, b, 0:256].bitcast(BF16)

        def mm(reads, writes, mms):
            def fn(e):
                inst = None
                for (o, l, r, st, sp) in mms:
                    inst = e.matmul(o, l, r, start=st, stop=sp)
                return inst
            kb.rec("pe", fn, reads, writes)

        def op(eng, reads, writes, f):
            kb.rec(eng, f, reads, writes)

        def dma(eng, semkey, reads, writes, out, in_, **kw):
            kb.rec(eng, lambda e: e.dma_start(out=out, in_=in_, **kw), reads, writes, dma=semkey)

        dma("sp", "xl", [], [("x", 0, tb, hf) for tb in range(TB) for hf in range(2)], x_bufs[0][:],
            x_d[0:T, :].rearrange("(tb p) d -> p tb d", p=128))
        dma("sp", "cst", [], [("ident_f",)], ident_f[:], ident_d)
        dma("sp", "cst", [], [("oh",)], oh_sb[:], oh_d)
        dma("sp", "cst", [], [("rb",)], rb_sb[:], rb_d)
        dma("sp", "cst", [], [("esk",)], esk[:], snk_d.partition_broadcast(128))
        op("pool", [], [("rows",)], lambda e: e.memset(rows, 0.0))
        dma("sp", "cst", [("rows",)], [("rows", 0)], rows[0:CW, :], cdk_d)
        for r_, v_ in [(P_BIAS, cdb_d), (P_LNG, clg_d), (P_LNB, clb_d), (P_F1, f1n_d), (P_MX, mxn_d), (P_F2, f2n_d)]:
            dma("sp", "cst", [("rows",)], [("rows", r_)], rows[r_:r_ + 1, :], v_.rearrange("(o n) -> o n", o=1))
        for i_, v_ in enumerate([qn_d, qn_d, kn_d, kn_d]):
            dma("sp", "cst", [("rows",)], [("rows", 100 + i_)], rows[P_QK:P_QK + 1, 64 * i_:64 * i_ + 64],
                v_.rearrange("(o n) -> o n", o=1))
        ROWTOK = [("rows", 0)] + [("rows", r_) for r_ in range(P_BIAS, P_F2 + 1)] + [("rows", 100 + i_) for i_ in range(4)]
        op("dve", [("ident_f",)], [("ident_b",)], lambda e: e.tensor_copy(out=ident_b[:], in_=ident_f[:]))
        op("pool", [], [("ones_b",)], lambda e: e.memset(ones_b[:], 1.0))
        op("pool", [], [("bones_b",)], lambda e: e.memset(bones_b[:], 0.0))
        op("pool", [("bones_b",)], [("bones_b",)], lambda e: e.memset(bones_b[0:64, 0:64], 1.0))
        op("pool", [("bones_b",)], [("bones_b",)], lambda e: e.memset(bones_b[64:128, 64:128], 1.0))
        for c in range(KC):
            b = bank()
            op("pe", ROWTOK + [("ident_f",)], [("ps", b)],
               lambda e, c=c, b=b: e.transpose(ps[:, b, 0:40], rows[0:40, c * 128:(c + 1) * 128], ident_f[0:40, 0:40]))
            op("dve", [("ps", b)], [("pvec",)], lambda e, c=c, b=b: e.tensor_copy(out=pvec[:, c, :], in_=ps[:, b, 0:40]))
        op("dve", [("pvec",)], [("gk8",)],
           lambda e: e.tensor_scalar(out=gk8[:], in0=pvec[:, 1, P_QK:P_QK + 1], scalar1=8.0, scalar2=None, op0=ALU.mult))
        op("act", [("esk",)], [("esk",)], lambda e: e.activation(out=esk[:], in_=esk[:], func=AF.Exp))
        op("pool", [], [("v", i) for i in range(TB + 1)], lambda e: e.memset(v_aug[:], 1.0))
        op("pool", [], [("zh", c) for c in range(KC)], lambda e: e.memset(zb_[:, :, 0:32], 0.0))
        op("pool", [], [("kn", h, 0) for h in range(4)], lambda e: e.memset(kn[:, :, 0:128], 0.0))
        b = bank()
        op("pe", [("oh",), ("rb",)], [("ps", b)],
           lambda e, b=b: e.matmul(ps[0:16, b, 0:128], rb_sb[:, :], oh_sb[:, :], start=True, stop=True))
        op("pool", [], [("arow",)], lambda e: e.memset(arow[:], 0.0))
        op("act", [("ps", b), ("arow",)], [("arow",)],
           lambda e, b=b: e.activation(out=arow[:, 127:255], in_=ps[0:16, b, 0:128], func=AF.Exp))
        EBTOK = lambda h, xy: [("EB", h, xy, role, gi) for role in range(2) for gi in range(2)]

        prep_n = [0]

        def prep(slab, out, in_, extra_reads=(), eng="pool"):
            semkey = "p%d" % (slab % NPREP)
            dma(eng, semkey, list(extra_reads), [("scr", slab, prep_n[0])], out, in_)
            scr_tok.setdefault(slab, []).append(("scr", slab, prep_n[0]))
            prep_n[0] += 1

        scr_tok = {}
        sv = lambda s_, n: wscr[s_].rearrange("p (k n) -> p k n", n=n)
        wv_ = lambda w: w.rearrange("(k p) n -> p k n", p=128)

        def prep_ffn(up0, dn0, wi, wo, jobs=None):
            wiv = wv_(wi)
            lst = []
            for s_ in range(11):
                lst.append((up0 + s_, sv(up0 + s_, 512)[:, 0:KC, 0:256], wiv[:, :, 256 * s_:256 * s_ + 256]))
                lst.append((up0 + s_, sv(up0 + s_, 512)[:, 0:KC, 256:512], wiv[:, :, DFF + 256 * s_:DFF + 256 * s_ + 256]))
            wov = wv_(wo)
            for half in range(2):
                for ks in range(3):
                    k0, k1 = 8 * ks, min(8 * ks + 8, FC)
                    lst.append((dn0 + half * 3 + ks, sv(dn0 + half * 3 + ks, 512)[:, 0:k1 - k0, :], wov[:, k0:k1, half * 512:(half + 1) * 512]))
            for a_ in lst:
                if jobs is None:
                    prep(*a_)
                else:
                    jobs.append(lambda a_=a_: prep(*a_))

        prep_ffn(F1_UP, F1_DN, f1wi_d, f1wo_d)
        def eb_part1():
            dma("sp", "misc", [("arow",)], [("a_d",)], a_d, arow[:])
            dma("sp", "misc", [("a_d",)], [("rbb",)],
                bass.AP(rbb_h, 0, [[128 * 383, 16], [383, 128], [1, 383]]),
                bass.AP(a_d.tensor, 0, [[383, 16], [0, 128], [1, 383]]))

        def eb_part2():
            for h in range(4):
                for xy in range(2):
                    for role in range(2):
                        c0 = 255 if role == 0 else 127
                        for gi in range(2):
                            j = 4 * h + 2 * gi + xy
                            col = (role * 2 + gi) * 128
                            dma("sp", "eb", [("rbb",)], [("EB", h, xy, role, gi)], EB[:, h, xy, col:col + 128],
                                bass.AP(rbb_h, j * 128 * 383 + c0, [[382, 128], [1, 128]]))


        kb.rec("pool", None, reads=[tk for s_ in range(F1_UP, F1_UP + 17) for tk in scr_tok[s_]])
        winv = wv_(win_d)
        for s_ in range(4):
            prep(MX_CV + s_, sv(MX_CV + s_, 512)[:, 0:KC, 0:256], winv[:, :, 256 * s_:256 * s_ + 256])
            prep(MX_CV + s_, sv(MX_CV + s_, 512)[:, 0:KC, 256:512], winv[:, :, 1024 + 256 * s_:1024 + 256 * s_ + 256])
        for s_ in range(2):
            prep(MX_Q + s_, sv(MX_Q + s_, 512)[:, 0:KC, :], winv[:, :, 2048 + 512 * s_:2048 + 512 * s_ + 512])
        for i_ in range(8):
            prep(MX_K, sv(MX_K, 512)[:, 0:KC, 64 * i_:64 * i_ + 64], winv[:, :, 3072 + 64 * (i_ // 2):3072 + 64 * (i_ // 2) + 64])
        prep(MX_V, sv(MX_V, 512)[:, 0:KC, 0:256], winv[:, :, 3328:3584])
        cwpv = wv_(cwp_d)
        for half in range(2):
            base = MX_MG + 4 * half
            prep(base + 0, sv(base + 0, 512)[:, 0:KC, :], winv[:, :, 3584 + 512 * half:3584 + 512 * half + 512])
            prep(base + 1, sv(base + 1, 512)[:, 0:KC, :], winv[:, :, 4608 + 512 * half:4608 + 512 * half + 512])
            prep(base + 2, sv(base + 2, 512)[:, 0:KC, :], cwpv[:, :, 512 * half:512 * half + 512])
            for kc in range(KC):
                ha, hb = head_pair(kc)
                prep(base + 3, sv(base + 3, 512)[0:64, kc, :], awo_d[64 * ha:64 * ha + 64, 512 * half:512 * half + 512])
                prep(base + 3, sv(base + 3, 512)[64:128, kc, :], awo_d[64 * hb:64 * hb + 64, 512 * half:512 * half + 512])
        woutv = wv_(wout_d)
        for half in range(2):
            prep(MX_WO + half, sv(MX_WO + half, 512)[:, 0:KC, :], woutv[:, :, 512 * half:512 * half + 512])
        def diag_jobs():
            jobs = []
            for c in range(KC):
                for g4 in range(4):
                    def job(c=c, g4=g4):
                        ts = tfs()
                        stage = tf[:, ts, :].bitcast(BF16).rearrange("p (j n) -> p j n", n=128)
                        taps = range(8 * g4, min(8 * g4 + 8, CW))

                        def fnd(e, stage=stage, taps=taps, c=c):
                            inst = None
                            for jj, j in enumerate(taps):
                                inst = e.tensor_scalar(out=stage[:, jj, :], in0=ident_b[:], scalar1=pvec[:, c, j:j + 1],
                                                       scalar2=None, op0=ALU.mult)
                            return inst
                        op("dve", [("pvec",), ("ident_b",)], [("tf", ts)], fnd)
                        n_ = len(taps)
                        prep(MX_DG + c, sv(MX_DG + c, 128)[:, 8 * g4:8 * g4 + n_, :], stage[:, 0:n_, :],
                             extra_reads=[("tf", ts)], eng="act")
                    jobs.append(job)
            return jobs

        tile_sched = (list(range(F1_UP, F1_UP + 17)) + list(range(MX_CV, MX_CV + 4)) + [MX_Q, MX_Q + 1, MX_K, MX_V]
                      + list(range(MX_DG, MX_DG + 8)) + list(range(MX_MG, MX_MG + 8)) + [MX_WO, MX_WO + 1]
                      + list(range(F2_UP, F2_UP + 17)))
        if stop == 1:
            tile_sched = tile_sched[:17]
        elif stop == 2:
            tile_sched = tile_sched[:17 + 26]
        elif stop == 0:
            tile_sched = []
        sched = tile_sched * ntiles
        slab_ne = {}
        for s_ in range(NSLAB):
            slab_ne[s_] = SLAB_E
        for base in (F1_DN, F2_DN):
            for half in range(2):
                slab_ne[base + half * 3 + 2] = 6 * 512
        slab_ne[MX_V] = SLAB_E
        ring = {"pos": 0, "loaded": 0, "done": -1}
        PFMAX = 2

        def wget(slab, keep=False):
            pos = ring["pos"]
            assert sched[pos] == slab, (pos, sched[pos], slab)
            ring["pos"] += 1
            if not keep:
                ring["done"] = pos - 1
            while (ring["loaded"] <= min(pos + PFMAX, len(sched) - 1)
                   and (ring["loaded"] - R <= ring["done"] or ring["loaded"] <= pos)):
                i = ring["loaded"]
                assert i - R <= ring["done"], "weight ring too small"
                sl = i % R
                s2 = sched[i]
                ne = slab_ne[s2]
                pk = "p%d" % (s2 % NPREP)
                kb.last_w[("scrall", s2)] = (pk, kb.dma_cnt[pk], "dma")
                dma("sp", "w%d" % sl, [("scrall", s2)], [("w", sl)], wring[:, sl, 0:ne], wscr[s2, :, 0:ne])
                ring["loaded"] += 1
            return pos % R

        HTOK = [("h", c) for c in range(KC)]

        def rmsnorm(prow):
            x_tm = cur["x"]
            for tb in range(TB):
                op("act", [XT(tb, 0), XT(tb, 1)], [("pt", 0), ("pt", 1), ("ss", tb)],
                   lambda e, tb=tb: e.activation(out=junk, in_=x_tm[:, tb, :], func=AF.Square, accum_out=ss[:, tb:tb + 1]))
                op("act", [("ss", tb)], [("ssq", tb)],
                   lambda e, tb=tb: e.activation(out=ssq[:, tb:tb + 1], in_=ss[:, tb:tb + 1], func=AF.Ln, bias=float(D * EPS)))
                op("act", [("ssq", tb)], [("rstd", tb)],
                   lambda e, tb=tb: e.activation(out=rstd[:, tb:tb + 1], in_=ssq[:, tb:tb + 1], func=AF.Exp, scale=-0.5))
                op("dve", [XT(tb, 0), XT(tb, 1), ("rstd", tb)], XNT(tb),
                   lambda e, tb=tb: e.tensor_scalar(out=xn_v(tb), in0=x_tm[:, tb, :], scalar1=rstd[:, tb:tb + 1],
                                                    scalar2=None, op0=ALU.mult))
            for c in range(KC):
                b = bank()

                def fn(e, c=c, b=b):
                    inst = None
                    for tb in range(TB):
                        inst = e.transpose(psb(b)[:, tb * 128:(tb + 1) * 128], xn_v(tb)[:, c * 128:(c + 1) * 128], ident_b[:])
                    return inst
                op("pe", [tk for tb in range(TB) for tk in XNT(tb)] + [("ident_b",)], [("ps", b)], fn)
                op("dve", [("ps", b), ("pvec",)], [("h", c)],
                   lambda e, c=c, b=b: e.tensor_scalar(out=h_fm[:, c, :], in0=psb(b), scalar1=pvec[:, c, prow:prow + 1],
                                                       scalar2=float(math.sqrt(D)), op0=ALU.mult, op1=ALU.mult))

        def ffn(up0, dn0, extra=None):
            for s_ in range(11):
                sl = wget(up0 + s_)
                wv = wring[:, sl, :].rearrange("p (k n) -> p k n", n=512)
                for jj in range(2):
                    i = 2 * s_ + jj
                    ba, bb = bank(), bank()
                    mm([("w", sl)] + HTOK, [("ps", ba)],
                       [(ps[:, ba, :], wv[:, kc, jj * 128:(jj + 1) * 128], h_fm[:, kc, :], kc == 0, kc == KC - 1) for kc in range(KC)])
                    mm([("w", sl)] + HTOK, [("ps", bb)],
                       [(ps[:, bb, :], wv[:, kc, 256 + jj * 128:256 + (jj + 1) * 128], h_fm[:, kc, :], kc == 0, kc == KC - 1) for kc in range(KC)])
                    ts = tfs()
                    op("act", [("ps", ba)], [("tf", ts)],
                       lambda e, ba=ba, ts=ts: e.activation(out=tf[:, ts, :], in_=ps[:, ba, :], func=AF.Silu))
                    op("dve", [("ps", bb), ("tf", ts)], [("hid", i)],
                       lambda e, bb=bb, ts=ts, i=i: e.tensor_tensor(out=hid[:, i, :], in0=ps[:, bb, :], in1=tf[:, ts, :], op=ALU.mult))
                    if extra:
                        extra.pop(0)()
            while extra:
                extra.pop(0)()
            for half in range(2):
                banks = []
                for tb in range(TB):
                    b = bank()
                    reserved.add(b)
                    banks.append(b)
                for ks in range(3):
                    sl = wget(dn0 + half * 3 + ks)
                    wv = wring[:, sl, :].rearrange("p (k n) -> p k n", n=512)
                    k0, k1 = 8 * ks, min(8 * ks + 8, FC)
                    for tb in range(TB):
                        mm([("w", sl)] + [("hid", kc) for kc in range(k0, k1)], [("ps", banks[tb])],
                           [(ps[:, banks[tb], :], hid[:, kc, tb * 128:(tb + 1) * 128], wv[:, kc - k0, :], kc == 0, kc == FC - 1)
                            for kc in range(k0, k1)])
                for tb in range(TB):
                    b = banks[tb]
                    xs_ = cur["x"][:, tb, half * 512:(half + 1) * 512]
                    op("dve", [("ps", b), XT(tb, half)], [XT(tb, half)],
                       lambda e, b=b, xs_=xs_: e.scalar_tensor_tensor(out=xs_, in0=ps[:, b, :], scalar=0.5, in1=xs_,
                                                                      op0=ALU.mult, op1=ALU.add))
                    reserved.discard(b)

        def qk_stage_a(wv, col0):
            bq = bank()
            mm([("w", wv[1])] + HTOK, [("ps", bq)],
               [(ps[:, bq, :], wv[0][:, kc, col0:col0 + 128], h_fm[:, kc, :], kc == 0, kc == KC - 1) for kc in range(KC)])
            t2 = tbs()
            op("act", [("ps", bq)], [("tb", t2)], lambda e: e.activation(out=tbf[:, t2, :], in_=ps[:, bq, :], func=AF.Square))
            return bq, t2

        def qk_stage_b(st, gain_ap, out_ap, out_tok):
            bq, t2 = st
            bs = bank()
            mm([("tb", t2), ("bones_b",)], [("ps", bs)], [(ps[:, bs, :], bones_b[:], tbf[:, t2, :], True, True)])
            t3, t3b = tfs(), tfs()
            op("act", [("ps", bs)], [("tf", t3b)],
               lambda e: e.activation(out=tf[:, t3b, :], in_=ps[:, bs, :], func=AF.Ln, bias=float(64 * EPS)))
            op("act", [("tf", t3b)], [("tf", t3)],
               lambda e: e.activation(out=tf[:, t3, :], in_=tf[:, t3b, :], func=AF.Exp, scale=-0.5))
            op("dve", [("ps", bq), ("tf", t3), ("pvec",), ("gk8",)], out_tok,
               lambda e: e.scalar_tensor_tensor(out=out_ap, in0=ps[:, bq, :], scalar=gain_ap, in1=tf[:, t3, :],
                                                op0=ALU.mult, op1=ALU.mult))

        def mixer(t):
            for s_ in range(4):
                sl = wget(MX_CV + s_)
                wv = wring[:, sl, :].rearrange("p (k n) -> p k n", n=512)
                for jj in range(2):
                    c = 2 * s_ + jj
                    bg, ba = bank(), bank()
                    mm([("w", sl)] + HTOK, [("ps", bg)],
                       [(ps[:, bg, :], wv[:, kc, 256 + jj * 128:256 + (jj + 1) * 128], h_fm[:, kc, :], kc == 0, kc == KC - 1) for kc in range(KC)])
                    mm([("w", sl)] + HTOK, [("ps", ba)],
                       [(ps[:, ba, :], wv[:, kc, jj * 128:(jj + 1) * 128], h_fm[:, kc, :], kc == 0, kc == KC - 1) for kc in range(KC)])
                    ts = tfs()
                    op("act", [("ps", bg)], [("tf", ts)],
                       lambda e, bg=bg, ts=ts: e.activation(out=tf[:, ts, :], in_=ps[:, bg, :], func=AF.Sigmoid))
                    op("dve", [("ps", ba), ("tf", ts)], [("z", c)],
                       lambda e, ba=ba, ts=ts, c=c: e.tensor_tensor(out=zb_[:, c, 32:32 + T], in0=ps[:, ba, :], in1=tf[:, ts, :], op=ALU.mult))
            if mstop <= 1:
                return
            jobs = []
            for s_ in range(2):
                for jj in range(4):
                    c = 4 * s_ + jj
                    jobs.append((MX_Q + s_, jj * 128, pvec[:, 0, P_QK:P_QK + 1], qn[:, c, :], [("qn", c)]))
            for h in range(4):
                jobs.append((MX_K, h * 128, gk8[:, 0:1], kn[:, h, 128:128 + T], [("kn", h, 1 + tb) for tb in range(TB)]))
            cur_slab, sl, wv = None, None, None
            pend = []
            for (slab, col0, gain_ap, out_ap, out_tok) in jobs:
                if slab != cur_slab:
                    sl = wget(slab)
                    wv = wring[:, sl, :].rearrange("p (k n) -> p k n", n=512)
                    cur_slab = slab
                st = qk_stage_a((wv, sl), col0)
                pend.append((st, gain_ap, out_ap, out_tok))
                if len(pend) > 1:
                    qk_stage_b(*pend.pop(0))
            while pend:
                qk_stage_b(*pend.pop(0))
            if mstop <= 3:
                return
            sl = wget(MX_V)
            wv = wring[:, sl, :].rearrange("p (k n) -> p k n", n=512)
            for tb in range(TB):
                b = bank()
                mm([("w", sl)] + HTOK, [("ps", b)],
                   [(ps[:, b, 0:256], h_fm[:, kc, tb * 128:(tb + 1) * 128], wv[:, kc, 0:256], kc == 0, kc == KC - 1) for kc in range(KC)])
                def fnv(e, b=b, tb=tb):
                    inst = None
                    for h in range(4):
                        off = 0 if h % 2 == 0 else 64
                        inst = e.activation(out=v_aug[:, 1 + tb, h, off:off + 64], in_=ps[:, b, 64 * h:64 * h + 64], func=AF.Identity)
                    return inst
                op("act", [("ps", b)], [("v", 1 + tb)], fnv)
            if t == 0:
                eb_part2()
            b1 = bank()
            reserved.add(b1)
            b2 = bank()
            reserved.add(b2)

            def conv_a(c):
                sl = wget(MX_DG + c)
                dv = wring[:, sl, :].rearrange("p (j n) -> p j n", n=128)
                b = bank()
                mm([("w", sl), ("z", c), ("zh", c)], [("ps", b)],
                   [(ps[:, b, :], dv[:, j, :], zb_[:, c, 2 + j:2 + j + T], j == 0, j == CW - 1) for j in range(CW)])
                zc = hid[:, 2 * c:2 * c + 2, :].rearrange("p a n -> p (a n)").bitcast(F32)
                t1, t2 = tbs(), tbs()
                bias_ap = pvec[:, c, P_BIAS:P_BIAS + 1]
                op("act", [("ps", b), ("pvec",)], [("hid", 2 * c), ("hid", 2 * c + 1)],
                   lambda e: e.activation(out=zc, in_=ps[:, b, :], func=AF.Identity, bias=bias_ap))
                op("act", [("ps", b), ("pvec",)], [("tb", t1)],
                   lambda e: e.activation(out=tbf[:, t1, :], in_=ps[:, b, :], func=AF.Identity, bias=bias_ap))
                op("act", [("ps", b), ("pvec",)], [("tb", t2)],
                   lambda e: e.activation(out=tbf[:, t2, :], in_=ps[:, b, :], func=AF.Square, bias=bias_ap))
                return t1, t2

            def conv_b(c, t1, t2):
                mm([("tb", t1), ("ones_b",)], [("ps", b1)], [(ps[:, b1, :], ones_b[:], tbf[:, t1, :], c == 0, c == KC - 1)])
                mm([("tb", t2), ("ones_b",)], [("ps", b2)], [(ps[:, b2, :], ones_b[:], tbf[:, t2, :], c == 0, c == KC - 1)])

            def att_a(n, h, slot0):
                gblk = t * TB + n
                roles = [1] if gblk == 0 else [0, 1]
                lo = 256 if gblk == 0 else 0
                bxy = [bank(), bank()]
                mms = []
                for role in roles:
                    kcol = (n + role) * 128
                    for gi in range(2):
                        for xy in range(2):
                            j = 4 * h + 2 * gi + xy
                            cq, pb = j // 2, 64 * xy
                            col = (role * 2 + gi) * 128
                            mms.append((ps[:, bxy[xy], col:col + 128], kn[pb:pb + 64, h, kcol:kcol + 128],
                                        qn[pb:pb + 64, cq, n * 128:(n + 1) * 128], True, True))
                mm([("kn", h, n + r_) for r_ in roles] + [("qn", 2 * h), ("qn", 2 * h + 1)], [("ps", bxy[0]), ("ps", bxy[1])], mms)
                pts = []
                for xy in range(2):
                    t1 = tbs()
                    t2 = slot0 + xy
                    op("act", [("ps", bxy[xy])], [("tb", t1)],
                       lambda e, b=bxy[xy], t1=t1: e.activation(out=tbf[:, t1, lo:512], in_=ps[:, b, lo:512], func=AF.Exp))
                    op("pool", [("tb", t1)] + EBTOK(h, xy), [("pt", t2)],
                       lambda e, t1=t1, t2=t2, xy=xy: e.tensor_tensor(out=ptile[:, t2, lo:512], in0=tbf[:, t1, lo:512],
                                                                      in1=EB[:, h, xy, lo:512], op=ALU.mult))
                    pts.append(t2)
                return roles, pts

            def att_b(n, h, roles, pts):
                bo = bank()
                mms = []
                for xy in range(2):
                    for r_ in roles:
                        mms.append((ps[:, bo, xy * 256:(xy + 1) * 256], v_aug[:, n + r_, h, :],
                                    ptile[:, pts[xy], r_ * 256:(r_ + 1) * 256], r_ == roles[0], r_ == roles[-1]))
                mm([("pt", pts[0]), ("pt", pts[1])] + [("v", n + r_) for r_ in roles], [("ps", bo)], mms)
                ob = 0 if h % 2 == 0 else 64
                db = 64 - ob
                t3, t4 = tfs(), tfs()

                def fna(e):
                    inst = None
                    for i in range(4):
                        j = 4 * h + 2 * (i % 2) + (i // 2)
                        inst = e.activation(out=tf[ob:ob + 64, t3, i * 128:(i + 1) * 128],
                                            in_=ps[db:db + 64, bo, i * 128:(i + 1) * 128],
                                            func=AF.Ln, bias=esk[db:db + 64, j:j + 1])
                    return inst
                op("act", [("ps", bo), ("esk",)], [("tf", t3)], fna)
                op("act", [("tf", t3)], [("tf", t4)],
                   lambda e: e.activation(out=tf[ob:ob + 64, t4, :], in_=tf[ob:ob + 64, t3, :], func=AF.Exp, scale=-1.0))
                c0 = (0 if h < 2 else 4)
                op("dve", [("ps", bo), ("tf", t4)], [("o", c0 + g) for g in range(4)],
                   lambda e: e.tensor_tensor(
                       out=o_fm[ob:ob + 64, c0:c0 + 4, n * 128:(n + 1) * 128],
                       in0=ps[ob:ob + 64, bo, :].rearrange("p (g q) -> p g q", q=128),
                       in1=tf[ob:ob + 64, t4, :].rearrange("p (g q) -> p g q", q=128), op=ALU.mult))

            items = [(n, h) for n in range(TB) for h in range(4)]
            f2jobs = []
            if t == 0 and stop >= 3:
                prep_ffn(F2_UP, F2_DN, f2wi_d, f2wo_d, jobs=f2jobs)
            for c in range(KC):
                for _ in range(2):
                    if f2jobs:
                        f2jobs.pop(0)()
                ia, ib = items[2 * c], items[2 * c + 1]
                sa = att_a(ia[0], ia[1], 0)
                sb_ = att_a(ib[0], ib[1], 2)
                ct = conv_a(c)
                att_b(ia[0], ia[1], *sa)
                att_b(ib[0], ib[1], *sb_)
                conv_b(c, *ct)
            t3, t4 = tfs(), tfs()
            op("dve", [("ps", b1)], [("lnm",)],
               lambda e: e.tensor_scalar(out=lnm[:], in0=ps[:, b1, :], scalar1=1.0 / D, scalar2=None, op0=ALU.mult))
            op("dve", [("lnm",)], [("tf", t3)], lambda e: e.tensor_tensor(out=tf[:, t3, :], in0=lnm[:], in1=lnm[:], op=ALU.mult))
            op("dve", [("ps", b2), ("tf", t3)], [("tf", t4)],
               lambda e: e.scalar_tensor_tensor(out=tf[:, t4, :], in0=ps[:, b2, :], scalar=1.0 / D, in1=tf[:, t3, :],
                                                op0=ALU.mult, op1=ALU.subtract))
            t4b = tfs()
            op("act", [("tf", t4)], [("tf", t4b)],
               lambda e: e.activation(out=tf[:, t4b, :], in_=tf[:, t4, :], func=AF.Ln, bias=float(EPS)))
            op("act", [("tf", t4b)], [("lnr",)], lambda e: e.activation(out=lnr[:], in_=tf[:, t4b, :], func=AF.Exp, scale=-0.5))
            reserved.discard(b1)
            reserved.discard(b2)
            for c in range(KC):
                zc = hid[:, 2 * c:2 * c + 2, :].rearrange("p a n -> p (a n)").bitcast(F32)
                t5, t6 = tfs(), tfs()
                if f2jobs:
                    f2jobs.pop(0)()
                op("pool", [("hid", 2 * c), ("hid", 2 * c + 1), ("lnm",)], [("tf", t5)],
                   lambda e, zc=zc, t5=t5: e.tensor_tensor(out=tf[:, t5, :], in0=zc, in1=lnm[:], op=ALU.subtract))
                op("dve", [("tf", t5), ("lnr",)], [("tf", t6)],
                   lambda e, t5=t5, t6=t6: e.tensor_tensor(out=tf[:, t6, :], in0=tf[:, t5, :], in1=lnr[:], op=ALU.mult))
                op("act", [("tf", t6), ("pvec",)], [("cact", c)],
                   lambda e, c=c, t6=t6: e.activation(out=cact[:, c, :], in_=tf[:, t6, :], func=AF.Silu,
                                                      scale=pvec[:, c, P_LNG:P_LNG + 1], bias=pvec[:, c, P_LNB:P_LNB + 1]))
            if mstop <= 6:
                return
            for half in range(2):
                base = MX_MG + 4 * half
                sls = [wget(base + i, keep=(i > 0)) for i in range(4)]
                wvs = [wring[:, sl, :].rearrange("p (k n) -> p k n", n=512) for sl in sls]
                OTOK = [("o", c) for c in range(KC)]
                CTOK = [("cact", c) for c in range(KC)]

                def grp(i, f4, src, stok):
                    b = bank()
                    mm([("w", sls[i])] + stok, [("ps", b)],
                       [(ps[:, b, :], wvs[i][:, kc, f4 * 128:(f4 + 1) * 128], src[:, kc, :], kc == 0, kc == KC - 1) for kc in range(KC)])
                    return b

                def mg_x(f4):
                    p = f4 % 2
                    bgc = grp(0, f4, h_fm, HTOK)
                    bga = grp(1, f4, h_fm, HTOK)
                    bb_ = grp(3, f4, o_fm, OTOK)
                    t2 = tfs()
                    op("act", [("ps", bgc)], [("mgh", 2 * p)],
                       lambda e: e.activation(out=mgh[:, 2 * p, :], in_=ps[:, bgc, :], func=AF.Sigmoid))
                    op("act", [("ps", bga)], [("tf", t2)],
                       lambda e: e.activation(out=tf[:, t2, :], in_=ps[:, bga, :], func=AF.Sigmoid))
                    op("dve", [("ps", bb_), ("tf", t2)], [("mgh", 2 * p + 1)],
                       lambda e: e.tensor_tensor(out=mgh[:, 2 * p + 1, :], in0=ps[:, bb_, :], in1=tf[:, t2, :], op=ALU.mult))

                def mg_y(f4):
                    p = f4 % 2
                    fc = 4 * half + f4
                    ba_ = grp(2, f4, cact, CTOK)
                    t3 = tfs()
                    op("dve", [("ps", ba_), ("mgh", 2 * p)], [("tf", t3)],
                       lambda e: e.tensor_tensor(out=tf[:, t3, :], in0=ps[:, ba_, :], in1=mgh[:, 2 * p, :], op=ALU.mult))
                    op("pool", [("tf", t3), ("mgh", 2 * p + 1)], [("qn", fc)],
                       lambda e: e.tensor_tensor(out=qn[:, fc, :], in0=tf[:, t3, :], in1=mgh[:, 2 * p + 1, :], op=ALU.add))

                while half == 0 and f2jobs:
                    f2jobs.pop(0)()
                mg_x(0)
                mg_x(1)
                mg_y(0)
                mg_x(2)
                mg_y(1)
                mg_x(3)
                mg_y(2)
                mg_y(3)
            for half in range(2):
                sl = wget(MX_WO + half)
                wv = wring[:, sl, :].rearrange("p (k n) -> p k n", n=512)
                for tb in range(TB):
                    b = bank()
                    mm([("w", sl)] + [("qn", c) for c in range(KC)], [("ps", b)],
                       [(ps[:, b, :], qn[:, kc, tb * 128:(tb + 1) * 128], wv[:, kc, :], kc == 0, kc == KC - 1) for kc in range(KC)])
                    xs_ = cur["x"][:, tb, half * 512:(half + 1) * 512]
                    op("dve", [("ps", b), XT(tb, half)], [XT(tb, half)],
                       lambda e, b=b, xs_=xs_: e.tensor_tensor(out=xs_, in0=ps[:, b, :], in1=xs_, op=ALU.add))
            op("pool", [("kn", h, TB) for h in range(4)], [("kn", h, 0) for h in range(4)],
               lambda e: e.tensor_copy(out=kn[:, :, 0:128], in_=kn[:, :, T:T + 128]))
            op("pool", [("v", TB)], [("v", 0)], lambda e: e.tensor_copy(out=v_aug[:, 0, :, :], in_=v_aug[:, TB, :, :]))
            op("pool", [("z", c) for c in range(KC)], [("zh", c) for c in range(KC)],
               lambda e: e.tensor_copy(out=zb_[:, :, 0:32], in_=zb_[:, :, T:T + 32]))

        for t in range(ntiles):
            if stop >= 1:
                rmsnorm(P_F1)
                ffn(F1_UP, F1_DN, extra=(diag_jobs() if (t == 0 and stop >= 2) else None))
                if t == 0 and stop >= 2:
                    eb_part1()
            if t + 1 < ntiles:
                nb = 1 - cur["p"]
                dma("sp", "xl", [], [("x", nb, tb, hf) for tb in range(TB) for hf in range(2)], x_bufs[nb][:],
                    x_d[(t + 1) * T:(t + 2) * T, :].rearrange("(tb p) d -> p tb d", p=128))
            if stop >= 2:
                rmsnorm(P_MX)
                mixer(t)
            if stop >= 3:
                rmsnorm(P_F2)
                ffn(F2_UP, F2_DN)
            for tb in range(TB):
                dma("sp", "xs", [XT(tb, 0), XT(tb, 1)], [("out", t, tb)],
                    out_d[t * T + tb * 128:t * T + (tb + 1) * 128, :], cur["x"][:, tb, :])
            cur["p"] = 1 - cur["p"]
            cur["x"] = x_bufs[cur["p"]]
        kb.rec("sp", None, reads=[("out", t, tb) for t in range(ntiles) for tb in range(TB)])

        with nc.Block() as block:
            def emit(eng_name):
                def body(e):
                    for waits, fn, done in kb.ops[eng_name]:
                        for k, v in waits:
                            e.wait_ge(sems[k], v)
                        if fn is None:
                            continue
                        inst = fn(e)
                        inst.then_inc(sems[done[0]], 16 if done[2] == "dma" else 1)
                return body
            block.tensor(emit("pe"))
            block.scalar(emit("act"))
            block.vector(emit("dve"))
            block.gpsimd(emit("pool"))
            block.sync(emit("sp"))
    return nc


def t5_bucket_onehot():
    oh = np.zeros((32, 128), np.float32)
    for d in range(128):
        if d < 16:
            b = d
        else:
            dd = np.float32(max(d, 1))
            val = np.log(dd / np.float32(16)) / np.float32(math.log(128 / 16)) * np.float32(16)
            b = min(16 + int(np.float32(val)), 31)
        oh[b, d] = 1.0
    return oh


_NC_CACHE = {}


def kernel(**inputs):
    ncores = 8
    if "nc" not in _NC_CACHE:
        _NC_CACHE["nc"] = build()
    nc = _NC_CACHE["nc"]
    x = np.ascontiguousarray(np.asarray(inputs["x"], dtype=np.float32))
    consts = {"c_ident": np.eye(128, dtype=np.float32), "c_onehot": t5_bucket_onehot()}
    shared = {k: np.ascontiguousarray(np.asarray(v, dtype=np.float32)) for k, v in inputs.items() if k != "x"}
    in_maps = []
    for b in range(ncores):
        m = dict(shared)
        m.update(consts)
        m["x"] = x[b]
        in_maps.append(m)
    res = run_bass_kernel_spmd(nc, in_maps, core_ids=list(range(ncores)))
    return np.stack([np.asarray(r["out"], dtype=np.float32) for r in res.results], axis=0)
```
